# Optimizing a Trainium2 kernel written in Bass

```python
import jax, jax.numpy as jnp
from jax import lax
import numpy as np

D_MODEL = 1024
BATCH = 32
SEQ = 2048
DEPTH = 4

MEM_LEN = 256
HEAD_DIM = D_MODEL // 16
A_HEADS = 6
B_HEADS = 4
C_HEADS = 6
D_A = A_HEADS * HEAD_DIM
D_B = B_HEADS * HEAD_DIM
D_C = C_HEADS * HEAD_DIM
CONV_A_WIDTH = 31
CONV_C_WIDTH = 3
CHUNK = 128
IN_WIDTH = 2 * D_A + 2 * D_B + 3 * D_C
SPLITS = (D_A, 2 * D_A, 2 * D_A + D_B, 2 * D_A + 2 * D_B,
          2 * D_A + 2 * D_B + D_C, 2 * D_A + 2 * D_B + 2 * D_C)
X_HEADS = 4
X_HEAD_DIM = D_MODEL // X_HEADS
D_FF = 4 * D_MODEL
LN_EPS = 1e-5
DEEPNORM_ALPHA = (2.0 * DEPTH) ** 0.25
DEEPNORM_BETA = (8.0 * DEPTH) ** -0.25

kernel_name = 'hybrid_conv_gmlp_shortconv_deepnorm_trunk'


def _layer_norm(x, g, b):
    xf = x.astype(jnp.float32)
    mu = jnp.mean(xf, axis=-1, keepdims=True)
    var = jnp.mean(jnp.square(xf - mu), axis=-1, keepdims=True)
    y = (xf - mu) * lax.rsqrt(var + LN_EPS)
    return (y * g.astype(jnp.float32) + b.astype(jnp.float32)).astype(x.dtype)


def _causal_depthwise_conv(x, w):
    k, c = w.shape
    return lax.conv_general_dilated(
        x, w[:, None, :].astype(x.dtype), window_strides=(1,), padding=[(k - 1, 0)],
        dimension_numbers=('NWC', 'WIO', 'NWC'), feature_group_count=c)


def _chunked_spatial_gate(u, v, w_s, b_s):
    bsz, s, _ = v.shape
    vh = v.reshape(bsz, s // CHUNK, CHUNK, B_HEADS, HEAD_DIM)
    causal = jnp.tril(jnp.ones((CHUNK, CHUNK), dtype=bool))
    w = jnp.where(causal[None], w_s, jnp.zeros_like(w_s)).astype(v.dtype)
    mixed = jnp.einsum('hts,bcshd->bcthd', w, vh) + b_s.T[None, None, :, :, None].astype(v.dtype)
    return u * mixed.reshape(bsz, s, D_B)


def _hybrid_mixer(h, w_in, conv_a_w, conv_a_b, ln_a_g, ln_a_b, ln_v_g, ln_v_b, w_s, b_s, conv_c_w, w_out):
    proj = h @ w_in
    a_val, a_gate, b_u, b_v, c_b, c_c, c_x = jnp.split(proj, SPLITS, axis=-1)
    a = _causal_depthwise_conv(a_val * jax.nn.sigmoid(a_gate), conv_a_w) + conv_a_b
    a = jax.nn.swish(_layer_norm(a, ln_a_g, ln_a_b))
    u = jax.nn.gelu(b_u, approximate=False)
    v = _layer_norm(jax.nn.gelu(b_v, approximate=False), ln_v_g, ln_v_b)
    bo = _chunked_spatial_gate(u, v, w_s, b_s)
    co = c_b * _causal_depthwise_conv(c_c * c_x, conv_c_w)
    return jnp.concatenate([a, bo, co], axis=-1) @ w_out


def _memory_cross_attention(h, mem, w_q, w_kv, w_o):
    bsz, s, _ = h.shape
    m = mem.shape[1]
    q = (h @ w_q).reshape(bsz, s, X_HEADS, X_HEAD_DIM)
    k, v = jnp.split(mem @ w_kv, 2, axis=-1)
    k = k.reshape(bsz, m, X_HEADS, X_HEAD_DIM)
    v = v.reshape(bsz, m, X_HEADS, X_HEAD_DIM)
    scores = jnp.einsum('bshd,bmhd->bhsm', q.astype(jnp.float32), k.astype(jnp.float32)) * (X_HEAD_DIM ** -0.5)
    p = jax.nn.softmax(scores, axis=-1).astype(h.dtype)
    o = jnp.einsum('bhsm,bmhd->bshd', p, v).reshape(bsz, s, D_MODEL)
    return o @ w_o


def _sq_relu_mlp(h, w_ff1, w_ff2):
    return jnp.square(jax.nn.relu(h @ w_ff1)) @ w_ff2


def setup_inputs(seed: int = 0) -> dict:
    key = jax.random.key(seed)
    ks = jax.random.split(key, 26)

    def nrm(k, shape, scale):
        return jax.random.normal(k, shape, dtype=jnp.float32) * scale

    def gain(k, n):
        return 1.0 + nrm(k, (DEPTH, n), 0.02)

    w_kv = jnp.concatenate([nrm(ks[16], (DEPTH, D_MODEL, D_MODEL), D_MODEL ** -0.5),
                            nrm(ks[17], (DEPTH, D_MODEL, D_MODEL), D_MODEL ** -0.5 * DEEPNORM_BETA)], axis=-1)
    return {
        'x': nrm(ks[0], (BATCH, SEQ, D_MODEL), 1.0),
        'mem': nrm(ks[1], (BATCH, MEM_LEN, D_MODEL), 1.0),
        'w_in': nrm(ks[2], (DEPTH, D_MODEL, IN_WIDTH), D_MODEL ** -0.5),
        'conv_a_w': nrm(ks[3], (DEPTH, CONV_A_WIDTH, D_A), CONV_A_WIDTH ** -0.5),
        'conv_a_b': nrm(ks[4], (DEPTH, D_A), 0.02),
        'ln_a_g': gain(ks[5], D_A),
        'ln_a_b': nrm(ks[6], (DEPTH, D_A), 0.02),
        'ln_v_g': gain(ks[7], D_B),
        'ln_v_b': nrm(ks[8], (DEPTH, D_B), 0.02),
        'w_s': nrm(ks[9], (DEPTH, B_HEADS, CHUNK, CHUNK), CHUNK ** -0.5),
        'b_s': 1.0 + nrm(ks[10], (DEPTH, B_HEADS, CHUNK), 0.02),
        'conv_c_w': nrm(ks[11], (DEPTH, CONV_C_WIDTH, D_C), CONV_C_WIDTH ** -0.5),
        'w_out': nrm(ks[12], (DEPTH, D_MODEL, D_MODEL), D_MODEL ** -0.5 * DEEPNORM_BETA),
        'ln1_g': gain(ks[13], D_MODEL),
        'ln1_b': nrm(ks[14], (DEPTH, D_MODEL), 0.02),
        'w_q': nrm(ks[15], (DEPTH, D_MODEL, D_MODEL), D_MODEL ** -0.5),
        'w_kv': w_kv,
        'w_o': nrm(ks[18], (DEPTH, D_MODEL, D_MODEL), D_MODEL ** -0.5 * DEEPNORM_BETA),
        'ln2_g': gain(ks[19], D_MODEL),
        'ln2_b': nrm(ks[20], (DEPTH, D_MODEL), 0.02),
        'w_ff1': nrm(ks[21], (DEPTH, D_MODEL, D_FF), D_MODEL ** -0.5),
        'w_ff2': nrm(ks[22], (DEPTH, D_FF, D_MODEL), D_FF ** -0.5 * DEEPNORM_BETA),
        'ln3_g': gain(ks[23], D_MODEL),
        'ln3_b': nrm(ks[24], (DEPTH, D_MODEL), 0.02),
    }


def reference(x, mem, w_in, conv_a_w, conv_a_b, ln_a_g, ln_a_b, ln_v_g, ln_v_b, w_s, b_s, conv_c_w, w_out,
              ln1_g, ln1_b, w_q, w_kv, w_o, ln2_g, ln2_b, w_ff1, w_ff2, ln3_g, ln3_b):
    for l in range(DEPTH):
        mix = _hybrid_mixer(x, w_in[l], conv_a_w[l], conv_a_b[l], ln_a_g[l], ln_a_b[l],
                            ln_v_g[l], ln_v_b[l], w_s[l], b_s[l], conv_c_w[l], w_out[l])
        x = _layer_norm(DEEPNORM_ALPHA * x + mix, ln1_g[l], ln1_b[l])
        att = _memory_cross_attention(x, mem, w_q[l], w_kv[l], w_o[l])
        x = _layer_norm(DEEPNORM_ALPHA * x + att, ln2_g[l], ln2_b[l])
        ff = _sq_relu_mlp(x, w_ff1[l], w_ff2[l])
        x = _layer_norm(DEEPNORM_ALPHA * x + ff, ln3_g[l], ln3_b[l])
    return x
```

```python
import numpy as np
from contextlib import ExitStack
import concourse.bass as bass
import concourse.mybir as mybir
from concourse.bass_utils import run_bass_kernel_spmd

F32 = mybir.dt.float32
BF16 = mybir.dt.bfloat16
AF = mybir.ActivationFunctionType
ALU = mybir.AluOpType
AX = mybir.AxisListType

D = 1024
S = 2048
T = 1024
NCH = T // 128
MEM = 256
DEPTH = 4
ALPHA = (2.0 * DEPTH) ** 0.25
EPS = 1e-5
NSLOT = 4
NPF = 6
ENGS = ['pe', 'act', 'dve', 'pool', 'sp']
CENGS = ['pe', 'act', 'dve', 'pool']
CH = 16000
U_KINDS = {'glu', 'prod', 'dg', 'dgc', 'vz', 'vzc', 'u', 'vg', 'a', 'asq', 'sigt', 'tmpb', 'mu', 'var', 'rsd',
           'qT', 'pT', 'pe', 'pn', 'KT', 'Vt', 'hT', 'rt', 'memf', 'memb', 'wsf', 'wsb', 'R1', 'R2', 'R3'}
ZONES = {'A': U_KINDS, 'B': {'nbh', 'a', 'asq'}}


class Prog:
    def __init__(self):
        self.q = {e: [] for e in ENGS}
        self.cnt = {e: 0 for e in ENGS}
        self.lastw = {}
        self.readers = {}
        self.seen = {e: {} for e in ENGS}
        self.dma_cnt = {}
        self.u_users = {z: {} for z in ZONES}
        self.u_deps = {z: {} for z in ZONES}
        self.dbg = []
        self.tag = ''

    def umark(self, z='A'):
        for k, v in self.u_users[z].items():
            if self.u_deps[z].get(k, 0) < v:
                self.u_deps[z][k] = v
        self.u_users[z] = {}

    def op(self, eng, fn, reads=(), writes=(), stream=None):
        kinds = set((r[0] if isinstance(r, tuple) else r) for r in list(reads) + list(writes))
        zt = [z for z in ZONES if kinds & ZONES[z]]
        deps = {}
        for z in zt:
            for k, v in self.u_deps[z].items():
                if deps.get(k, 0) < v:
                    deps[k] = v
        for r in reads:
            t = self.lastw.get(r)
            if t is not None and deps.get(t[0], 0) < t[1]:
                deps[t[0]] = t[1]
        for r in writes:
            t = self.lastw.get(r)
            if t is not None and deps.get(t[0], 0) < t[1]:
                deps[t[0]] = t[1]
            for k, v in self.readers.get(r, {}).items():
                if deps.get(k, 0) < v:
                    deps[k] = v
        if fn is None:
            me = None
        elif stream is None:
            self.cnt[eng] += 1
            me = (eng, self.cnt[eng])
        else:
            key = ('dma', stream)
            self.dma_cnt[key] = self.dma_cnt.get(key, 0) + 1
            me = (key, self.dma_cnt[key])
        waits = []
        seen = self.seen[eng]
        for k, v in deps.items():
            if k == 'pe' and eng == 'pe':
                continue
            if seen.get(k, 0) >= v:
                continue
            seen[k] = v
            waits.append((k, v))
        self.q[eng].append((waits, fn, me))
        self.dbg.append((eng, self.tag, me, tuple(waits)))
        if me is not None:
            for z in zt:
                if self.u_users[z].get(me[0], 0) < me[1]:
                    self.u_users[z][me[0]] = me[1]
        if me is not None:
            for r in reads:
                d = self.readers.setdefault(r, {})
                if d.get(me[0], 0) < me[1]:
                    d[me[0]] = me[1]
            for r in writes:
                self.lastw[r] = me
                self.readers[r] = {}
        return me

    def barrier(self):
        snap = {e: self.cnt[e] for e in CENGS}
        for e in ENGS:
            waits = []
            for k, v in snap.items():
                if v == 0 or self.seen[e].get(k, 0) >= v:
                    continue
                self.seen[e][k] = v
                waits.append((k, v))
            if waits:
                self.q[e].append((waits, None, None))

    def emit(self, nc, stack):
        needed = set()
        for e in ENGS:
            for waits, fn, me in self.q[e]:
                for w in waits:
                    needed.add(w)
        sig = {}
        nsig = {}
        for e in ENGS:
            n = 0
            for waits, fn, me in self.q[e]:
                if me is not None and me[0] == e and me in needed:
                    n += 1
                    sig[me] = n
            nsig[e] = n
        sems = {}
        for e in ENGS:
            sems[e] = [stack.enter_context(nc.semaphore(f"s_{e}_{i}")) for i in range((nsig[e] + CH - 1) // CH)]
        dsems = {k: stack.enter_context(nc.semaphore("d_" + str(k[1]))) for k in self.dma_cnt}

        def semval(k, v):
            if isinstance(k, tuple):
                return dsems[k], 16 * v
            idx = sig[(k, v)]
            return sems[k][(idx - 1) // CH], (idx - 1) % CH + 1

        block = stack.enter_context(nc.Block())
        q = self.q

        def run(engname):
            def body(eng):
                for waits, fn, me in q[engname]:
                    for k, v in waits:
                        s, val = semval(k, v)
                        eng.wait_ge(s, val)
                    if fn is None:
                        continue
                    ins = fn(eng)
                    if isinstance(me[0], tuple):
                        ins.then_inc(dsems[me[0]], 16)
                    elif me in sig:
                        s, val = semval(*me)
                        ins.then_inc(s, 1)
            return body
        block.tensor(run('pe'))
        block.scalar(run('act'))
        block.vector(run('dve'))
        block.gpsimd(run('pool'))
        block.sync(run('sp'))


def weight_specs(nl):
    sp = []
    for l in range(nl):
        for b in (0, 1, 2, 4, 5, 3):
            sp.append(('in', l, b))
        for hf in range(2):
            sp.append(('out', l, hf))
        for b in range(2):
            sp.append(('k', l, b))
        for b in range(2):
            sp.append(('v', l, b))
        for b in range(2):
            sp.append(('q', l, b))
        for hf in range(2):
            sp.append(('o', l, hf))
        for g in range(2):
            for b in range(4):
                sp.append(('f1', l, g, b))
            for cb in range(2):
                for kk in range(2):
                    sp.append(('f2', l, g, cb, kk))
    return sp


IN_COLS = {0: (0, 384), 1: (384, 384), 2: (768, 512), 3: (1280, 384), 4: (1664, 384), 5: (2048, 384)}


def build(nseq, nl, S=S):
    nc = bass.Bass("TRN2", target_bir_lowering=False)
    NT = nseq * S // T
    TPS = S // T

    def din(name, shape):
        return nc.dram_tensor(name, shape, F32, kind="ExternalInput").ap()
    x = din("x", [nseq * S, D])
    mem = din("mem", [nseq * MEM, D])
    w_in = din("w_in", [DEPTH, D, 2432])
    conv_a_w = din("conv_a_w", [DEPTH * 31, 384])
    conv_a_b = din("conv_a_b", [DEPTH, 384])
    ln_a_g = din("ln_a_g", [DEPTH, 384])
    ln_a_b = din("ln_a_b", [DEPTH, 384])
    ln_v_g = din("ln_v_g", [DEPTH, 256])
    ln_v_b = din("ln_v_b", [DEPTH, 256])
    w_s = din("w_s", [DEPTH, 4, 128, 128])
    b_s = din("b_s", [DEPTH, 4, 128])
    conv_c_w = din("conv_c_w", [DEPTH * 3, 384])
    w_out = din("w_out", [DEPTH, D, D])
    lng = [din("ln1_g", [DEPTH, D]), din("ln1_b", [DEPTH, D]), din("ln2_g", [DEPTH, D]),
           din("ln2_b", [DEPTH, D]), din("ln3_g", [DEPTH, D]), din("ln3_b", [DEPTH, D])]
    w_q = din("w_q", [DEPTH, D, D])
    w_kv = din("w_kv", [DEPTH, D, 2 * D])
    w_o = din("w_o", [DEPTH, D, D])
    w_ff1 = din("w_ff1", [DEPTH, D, 4 * D])
    w_ff2 = din("w_ff2", [DEPTH, 4 * D, D])
    out = nc.dram_tensor("out", [nseq * S, D], F32, kind="ExternalOutput").ap()

    stack = ExitStack()
    with stack:
        def sb(name, shape, dt):
            return stack.enter_context(nc.sbuf_tensor(name, shape, dt))
        x_tm = sb("x_tm", [128, NCH, D], F32)
        xT = sb("xT", [128, 8, T], BF16)
        nb = sb("nb", [128, 4, D], BF16)
        cat = sb("cat", [128, 8, T], BF16)
        ring = [sb(f"ring{i}", [128, 8, 512], BF16) for i in range(NSLOT)]
        gbt = sb("gbt", [128, 2, D], F32)
        memT = sb("memT", [128, 8, MEM], BF16)
        wsT = sb("wsT", [128, 16, 128], BF16)
        gbT = sb("gbT", [128, 8, 24], F32)
        caw = sb("caw", [128, 3, 124], F32)
        misc = sb("misc", [128, 3, 24], F32)
        lvgb = sb("lvgb", [128, 2, 256], F32)
        bst = sb("bst", [128, 2, 128], F32)
        identb = sb("identb", [128, 128], BF16)
        identf = sb("identf", [128, 128], F32)
        onesf = sb("onesf", [128, 128], F32)
        maskc = sb("maskc", [128, 128], F32)
        epst = sb("epst", [128, 1], F32)
        gstate = sb("gstate", [128, DEPTH, 3, 30], BF16)
        pstate = sb("pstate", [128, DEPTH, 3, 2], BF16)
        lnst = sb("lnst", [128, NCH, 12], F32)
        lnmv = sb("lnmv", [128, NCH, 2], F32)
        lnsd = sb("lnsd", [128, NCH, 3], F32)
        vst = sb("vst", [128, 4, 6], F32)
        vmv = sb("vmv", [128, 4, 4], F32)
        amx = sb("amx", [128, 2, 12], F32)
        asum = sb("asum", [128, 2, 8], F32)
        scr = sb("scr", [128, 2], F32)
        UW = 37248
        U = sb("U", [128, UW], BF16)
        pf = [stack.enter_context(nc.psum_tensor(f"pf{i}", [128, 512], F32)) for i in range(NPF)]
        pb = [stack.enter_context(nc.psum_tensor(f"pb{i}", [128, 1024], BF16)) for i in range(2)]

        def ub(off, n):
            return U[:, off:off + n]

        def uf(off, n):
            return U[:, off:off + 2 * n].bitcast(F32)
        o = 0
        glu = ub(o, 3 * 1056).rearrange("p (i n) -> p i n", i=3); o += 3 * 1056
        prod = ub(o, 3 * 1028).rearrange("p (i n) -> p i n", i=3); o += 3 * 1028
        dg = ub(o, 93 * 128).rearrange("p (k n) -> p k n", k=93); o += 93 * 128
        dgc = ub(o, 9 * 128).rearrange("p (k n) -> p k n", k=9); o += 9 * 128
        vz = ub(o, 2048); o += 2048
        u_t = uf(o, 1024).rearrange("p (j n) -> p j n", j=2); o += 2048
        vg = uf(o, 1024).rearrange("p (c n) -> p c n", c=4); o += 2048
        a_t = uf(o, 1536).rearrange("p (i n) -> p i n", i=3); o += 3072
        asq = uf(o, 1536).rearrange("p (i n) -> p i n", i=3); o += 3072
        sigt = uf(o, 1024).rearrange("p (b n) -> p b n", b=2); o += 2048
        tmpb = uf(o, 256); o += 512
        mu_t = uf(o, 512); o += 1024
        var_t = uf(o, 512); o += 1024
        rsd_t = uf(o, 512); o += 1024
        assert o <= UW, o
        o = 0
        qT = ub(o, 8 * T).rearrange("p (f n) -> p f n", f=8); o += 8 * T
        pT = ub(o, 8192).rearrange("p (g h m n) -> p g h m n", g=2, h=4, m=2); o += 8192
        pe_ = uf(o, 2048).rearrange("p (b h n) -> p b h n", b=2, h=4); o += 4096
        pn = ub(o, 2048).rearrange("p (b h n) -> p b h n", b=2, h=4); o += 2048
        KT = ub(o, 2048).rearrange("p (f n) -> p f n", f=8); o += 2048
        Vt = ub(o, 2048).rearrange("p (m n) -> p m n", m=2); o += 2048
        assert o <= 26624
        nbh = ub(26624, 4096).rearrange("p (c n) -> p c n", c=4)
        o = 0
        hT = ub(o, 16 * T).rearrange("p (f n) -> p f n", f=16); o += 16 * T
        rt = uf(o, 1024).rearrange("p (b n) -> p b n", b=2); o += 2048
        assert o <= UW
        memf = uf(0, 2048).rearrange("p (m n) -> p m n", m=2)
        memb = ub(4096, 2048).rearrange("p (m n) -> p m n", m=2)
        wsf = uf(0, 2048).rearrange("p (k n) -> p k n", k=16)
        wsb = ub(4096, 2048).rearrange("p (k n) -> p k n", k=16)
        R1 = uf(8192, 1024)
        R2 = uf(8192 + 2048, 384)
        R3 = uf(8192 + 2048 + 768, 384)

        p = Prog()
        pfi = [0]

        def nextpf():
            b = pfi[0]
            pfi[0] = (b + 1) % NPF
            return b
        pbi = [0]

        def nextpb():
            b = pbi[0]
            pbi[0] = (b + 1) % 2
            return b

        def mm(o_, lhsT, rhs, start, stop, reads, writes):
            p.op('pe', lambda e: e.matmul(o_, lhsT=lhsT, rhs=rhs, start=start, stop=stop), reads, writes)

        def tr(o_, in_, ident, reads, writes):
            p.op('pe', lambda e: e.transpose(out=o_, in_=in_, identity=ident), reads, writes)

        def act(o_, in_, func, reads, writes, bias=None, scale=None, accum=None):
            kw = {}
            if bias is not None:
                kw['bias'] = bias
            if scale is not None:
                kw['scale'] = scale
            if accum is not None:
                kw['accum_out'] = accum
            p.op('act', lambda e: e.activation(out=o_, in_=in_, func=func, **kw), reads, writes)

        def tt(eng, o_, in0, in1, op, reads, writes):
            p.op(eng, lambda e: e.tensor_tensor(out=o_, in0=in0, in1=in1, op=op), reads, writes)

        def ts(eng, o_, in0, s1, s2, op0, op1, reads, writes):
            if op1 is None:
                p.op(eng, lambda e: e.tensor_scalar(out=o_, in0=in0, scalar1=s1, scalar2=None, op0=op0), reads, writes)
            else:
                p.op(eng, lambda e: e.tensor_scalar(out=o_, in0=in0, scalar1=s1, scalar2=s2, op0=op0, op1=op1), reads, writes)

        def stt(o_, in0, sc, in1, op0, op1, reads, writes):
            p.op('dve', lambda e: e.scalar_tensor_tensor(out=o_, in0=in0, scalar=sc, in1=in1, op0=op0, op1=op1), reads, writes)

        def cp(eng, o_, in_, reads, writes):
            p.op(eng, lambda e: e.tensor_copy(out=o_, in_=in_), reads, writes)

        def dma(eng, o_, in_, reads, writes, stream):
            p.op(eng, lambda e: e.dma_start(out=o_, in_=in_), reads, writes, stream=stream)

        specs = []
        for ti in range(NT):
            specs += weight_specs(nl)
        wstate = {'use': 0, 'iss': 0}

        def wsrc(key):
            kind, l = key[0], key[1]
            if kind == 'in':
                c0, n = IN_COLS[key[2]]
                return w_in[l, :, c0:c0 + n], n
            if kind == 'out':
                return w_out[l, :, key[2] * 512:(key[2] + 1) * 512], 512
            if kind == 'k':
                return w_kv[l, :, key[2] * 512:(key[2] + 1) * 512], 512
            if kind == 'v':
                return w_kv[l, :, 1024 + key[2] * 512:1024 + (key[2] + 1) * 512], 512
            if kind == 'q':
                return w_q[l, :, key[2] * 512:(key[2] + 1) * 512], 512
            if kind == 'o':
                return w_o[l, :, key[2] * 512:(key[2] + 1) * 512], 512
            if kind == 'f1':
                c0 = key[2] * 2048 + key[3] * 512
                return w_ff1[l, :, c0:c0 + 512], 512
            if kind == 'f2':
                r0 = key[2] * 2048 + key[4] * 1024
                return w_ff2[l, r0:r0 + 1024, key[3] * 512:(key[3] + 1) * 512], 512
            raise ValueError(key)

        def wissue(j):
            src, n = wsrc(specs[j])
            slot = j % NSLOT
            dma('pool', ring[slot][:, :, 0:n], src.rearrange("(k p) n -> p k n", p=128), [], [('w', slot)], f"w{slot}")

        def wget(key):
            i = wstate['use']
            assert specs[i] == key, (specs[i], key)
            lim = min(len(specs), i + NSLOT - 1)
            while wstate['iss'] < lim:
                wissue(wstate['iss'])
                wstate['iss'] += 1
            wstate['use'] += 1
            return i % NSLOT

        p.op('pool', lambda e: e.memset(identb[:], 1.0), [], ['identb'])
        p.op('pool', lambda e: e.affine_select(out=identb[:], in_=identb[:], pattern=[[-1, 128]], compare_op=ALU.is_equal,
                                                fill=0.0, base=0, channel_multiplier=1), ['identb'], ['identb'])
        p.op('pool', lambda e: e.memset(identf[:], 1.0), [], ['identf'])
        p.op('pool', lambda e: e.affine_select(out=identf[:], in_=identf[:], pattern=[[-1, 128]], compare_op=ALU.is_equal,
                                                fill=0.0, base=0, channel_multiplier=1), ['identf'], ['identf'])
        p.op('pool', lambda e: e.memset(onesf[:], 1.0), [], ['onesf'])
        p.op('pool', lambda e: e.memset(epst[:], EPS), [], ['epst'])
        p.op('pool', lambda e: e.memset(maskc[:], 1.0), [], ['maskc'])
        p.op('pool', lambda e: e.affine_select(out=maskc[:], in_=maskc[:], pattern=[[1, 128]], compare_op=ALU.is_ge,
                                                fill=0.0, base=0, channel_multiplier=-1), ['maskc'], ['maskc'])
        for i6 in range(6):
            dma('sp', R1[i6 * 4:(i6 + 1) * 4, :], lng[i6], [], [('R1', i6)], f'parR1_{i6}')
        dma('sp', R2[0:124, :], conv_a_w, [], ['R2'], 'parR2')
        dma('sp', R3[0:4, :], conv_a_b, [], [('R3', 0)], 'parR30')
        dma('sp', R3[4:8, :], ln_a_g, [], [('R3', 1)], 'parR31')
        dma('sp', R3[8:12, :], ln_a_b, [], [('R3', 2)], 'parR32')
        dma('sp', R3[12:24, :], conv_c_w, [], [('R3', 3)], 'parR33')
        dma('sp', wsf, w_s.rearrange("l h t s -> t (l h) s"), [], ['wsf'], 'parws')
        for f in range(8):
            b = nextpf()
            tr(pf[b][:, 0:24], R1[0:24, f * 128:(f + 1) * 128], identf[0:24, 0:24],
               [('R1', i6) for i6 in range(6)] + ['identf'], [('pf', b)])
            cp('dve', gbT[:, f, :], pf[b][:, 0:24], [('pf', b)], ['gbT'])
        for i in range(3):
            b = nextpf()
            tr(pf[b][:, 0:124], R2[0:124, i * 128:(i + 1) * 128], identf[0:124, 0:124], ['R2', 'identf'], [('pf', b)])
            cp('dve', caw[:, i, :], pf[b][:, 0:124], [('pf', b)], ['caw'])
            b = nextpf()
            tr(pf[b][:, 0:24], R3[0:24, i * 128:(i + 1) * 128], identf[0:24, 0:24],
               [('R3', k) for k in range(4)] + ['identf'], [('pf', b)])
            cp('dve', misc[:, i, :], pf[b][:, 0:24], [('pf', b)], ['misc'])
        cp('dve', wsb, wsf, ['wsf'], ['wsb'])
        for q8 in range(2):
            b = nextpb()
            for k in range(8):
                tr(pb[b][:, k * 128:(k + 1) * 128], wsb[:, q8 * 8 + k, :], identb[:], ['wsb', 'identb'], [('pb', b)])
            for k in range(8):
                tt('dve', wsT[:, q8 * 8 + k, :], pb[b][:, k * 128:(k + 1) * 128], maskc[:], ALU.mult,
                   [('pb', b), 'maskc'], ['wsT'])

        def umark(z='A'):
            p.umark(z)

        def load_gbt(l, n):
            dma('sp', gbt[:, 0, :], lng[2 * n][l:l + 1, :].partition_broadcast(128), [], [('gbt', 0)], 'gbt0')
            dma('sp', gbt[:, 1, :], lng[2 * n + 1][l:l + 1, :].partition_broadcast(128), [], [('gbt', 1)], 'gbt1')

        def nbs(c):
            return nb[:, c, :] if c < 4 else nbh[:, c - 4, :]

        def nbr(c):
            return ('nb', c) if c < 4 else ('nbh', c)

        def make_xT(h, gcol, bcol):
            for fp in range(4):
                b = nextpb()
                for f in (2 * fp, 2 * fp + 1):
                    off = (f % 2) * 512
                    for cc in range(4):
                        c = 4 * h + cc
                        tr(pb[b][:, off + cc * 128:off + (cc + 1) * 128], nbs(c)[:, f * 128:(f + 1) * 128], identb[:],
                           [nbr(c), 'identb'], [('pb', b)])
                for f in (2 * fp, 2 * fp + 1):
                    off = (f % 2) * 512
                    wr = [('xT', 4 * h + cc) for cc in range(4)]
                    if gcol is None:
                        if f % 2 == 0:
                            act(xT[:, f, h * 512:(h + 1) * 512], pb[b][:, off:off + 512], AF.Copy, [('pb', b)], wr)
                        else:
                            cp('dve', xT[:, f, h * 512:(h + 1) * 512], pb[b][:, off:off + 512], [('pb', b)], wr)
                    elif f % 2 == 0:
                        act(xT[:, f, h * 512:(h + 1) * 512], pb[b][:, off:off + 512], AF.Identity,
                            [('pb', b), 'gbT'], wr,
                            bias=gbT[:, f, bcol:bcol + 1], scale=gbT[:, f, gcol:gcol + 1])
                    else:
                        ts('dve', xT[:, f, h * 512:(h + 1) * 512], pb[b][:, off:off + 512],
                           gbT[:, f, gcol:gcol + 1], gbT[:, f, bcol:bcol + 1], ALU.mult, ALU.add,
                           [('pb', b), 'gbT'], wr)

        def ln_s1(c):
            R = [('x', c)]
            L = [('lnst', c)]
            p.op('dve', lambda e: e.bn_stats(out=lnst[:, c, 0:6], in_=x_tm[:, c, 0:512]), R, L)
            p.op('dve', lambda e: e.bn_stats(out=lnst[:, c, 6:12], in_=x_tm[:, c, 512:1024]), R, L)
            p.op('dve', lambda e: e.bn_aggr(out=lnmv[:, c, :], in_=lnst[:, c, :]), L, L)
            act(lnsd[:, c, 0:1], lnmv[:, c, 1:2], AF.Sqrt, L + ['epst'], L, bias=epst[:, 0:1])

        def ln_s2(c, want_nb):
            R = [('x', c)]
            L = [('lnst', c)]
            p.op('dve', lambda e: e.reciprocal(out=lnsd[:, c, 1:2], in_=lnsd[:, c, 0:1]), L, L)
            ts('dve', lnsd[:, c, 2:3], lnmv[:, c, 0:1], -1.0, lnsd[:, c, 1:2], ALU.mult, ALU.mult, L, L)
            if want_nb:
                act(nbs(c), x_tm[:, c, :], AF.Identity, R + L, [nbr(c)],
                    bias=lnsd[:, c, 2:3], scale=lnsd[:, c, 1:2])

        pdefer = []

        def flush_pdefer(n=None):
            k = len(pdefer) if n is None else min(n, len(pdefer))
            for _ in range(k):
                pdefer.pop(0)()

        def ln_s3(c, fast=False):
            xs = x_tm[:, c, :]
            R = [('x', c)]
            L = [('lnst', c)]
            if fast:
                act(xs, xs, AF.Identity, R + L, R, bias=lnsd[:, c, 2:3], scale=lnsd[:, c, 1:2])
                tt('dve', xs, xs, gbt[:, 0, :], ALU.mult, R + [('gbt', 0)], R)
                tt('pool', xs, xs, gbt[:, 1, :], ALU.add, R + [('gbt', 1)], R)
            else:
                def later():
                    ts('pool', xs, xs, lnsd[:, c, 1:2], lnsd[:, c, 2:3], ALU.mult, ALU.add, R + L, R)
                    tt('pool', xs, xs, gbt[:, 0, :], ALU.mult, R + [('gbt', 0)], R)
                    tt('pool', xs, xs, gbt[:, 1, :], ALU.add, R + [('gbt', 1)], R)
                pdefer.append(later)

        class LNPipe:
            def __init__(self, want_nb, xt_args, done=None):
                self.want_nb = want_nb
                self.xt_args = xt_args
                self.done = done

            def _s3(self, c):
                ln_s3(c, fast=(self.done is not None))
                if self.done is not None:
                    self.done(c)

            def push(self, c):
                ln_s1(c)
                if c >= 1:
                    ln_s2(c - 1, self.want_nb)
                if c >= 2:
                    self._s3(c - 2)

            def flush(self):
                ln_s2(NCH - 1, self.want_nb)
                self._s3(NCH - 2)
                self._s3(NCH - 1)

        def xT_res(h):
            return [('xT', 4 * h + cc) for cc in range(4)]

        def cat_res(c):
            return [('cat', f, c) for f in range(8)]

        def proj_out(kind, l, after):
            for hf in range(2):
                s = wget((kind, l, hf))
                for c in range(NCH):
                    b = nextpf()
                    for k in range(8):
                        mm(pf[b][:, :], cat[:, k, c * 128:(c + 1) * 128], ring[s][:, k, 0:512], k == 0, k == 7,
                           [('w', s)] + cat_res(c), [('pf', b)])
                    xs = x_tm[:, c, hf * 512:(hf + 1) * 512]
                    stt(xs, xs, ALPHA, pf[b][:, :], ALU.mult, ALU.add, [('x', c), ('pf', b)], [('x', c)])
                    after(hf, c)

        def k_proj(l):
            for blk in range(2):
                s = wget(('k', l, blk))
                for j in range(4):
                    b = nextpf()
                    for k in range(8):
                        mm(pf[b][:, 0:256], ring[s][:, k, j * 128:(j + 1) * 128], memT[:, k, :], k == 0, k == 7,
                           [('w', s), 'memT'], [('pf', b)])
                    if j % 2 == 0:
                        act(KT[:, blk * 4 + j, :], pf[b][:, 0:256], AF.Copy, [('pf', b)], ['KT'])
                    else:
                        cp('dve', KT[:, blk * 4 + j, :], pf[b][:, 0:256], [('pf', b)], ['KT'])

        def v_proj(l):
            for blk in range(2):
                s = wget(('v', l, blk))
                for mc in range(2):
                    b = nextpf()
                    for k in range(8):
                        mm(pf[b][:, :], memT[:, k, mc * 128:(mc + 1) * 128], ring[s][:, k, 0:512], k == 0, k == 7,
                           [('w', s), 'memT'], [('pf', b)])
                    if mc == 0:
                        cp('dve', Vt[:, mc, blk * 512:(blk + 1) * 512], pf[b][:, :], [('pf', b)], ['Vt'])
                    else:
                        act(Vt[:, mc, blk * 512:(blk + 1) * 512], pf[b][:, :], AF.Copy, [('pf', b)], ['Vt'])

        umark()
        for ti in range(NT):
            seq = ti // TPS
            first = (ti % TPS == 0)
            row0 = ti * T
            for c in range(NCH):
                dma('sp', x_tm[:, c, :], x[row0 + c * 128:row0 + (c + 1) * 128, :], [], [('x', c)], f'xin{c}')
            if first:
                dma('sp', memf, mem[seq * MEM:(seq + 1) * MEM, :].rearrange("(m p) d -> p m d", p=128), [], ['memf'], 'mem')
                cp('dve', memb, memf, ['memf'], ['memb'])
                for q4 in range(2):
                    b = nextpb()
                    for fq in range(4):
                        for mc in range(2):
                            f = q4 * 4 + fq
                            tr(pb[b][:, (fq * 2 + mc) * 128:(fq * 2 + mc + 1) * 128], memb[:, mc, f * 128:(f + 1) * 128],
                               identb[:], ['memb', 'identb'], [('pb', b)])
                    act(memT[:, q4 * 4:(q4 + 1) * 4, :].rearrange("p f n -> p (f n)"), pb[b][:, :], AF.Copy,
                        [('pb', b)], ['memT'])
                umark()
            for h in range(2):
                for cc in range(4):
                    c = 4 * h + cc
                    if cc % 2 == 0:
                        cp('dve', nbs(c), x_tm[:, c, :], [('x', c)], [nbr(c)])
                    else:
                        act(nbs(c), x_tm[:, c, :], AF.Copy, [('x', c)], [nbr(c)])
            make_xT(0, None, None)
            pend = {'x1': (None, None)}

            for l in range(nl):
                lastl = (l == nl - 1)
                dma('sp', lvgb[:, 0, :], ln_v_g[l:l + 1, :].partition_broadcast(128), [], ['lvgb'], 'lv0')
                dma('sp', lvgb[:, 1, :], ln_v_b[l:l + 1, :].partition_broadcast(128), [], ['lvgb'], 'lv1')
                for h4 in range(4):
                    hp, j = h4 % 2, h4 // 2
                    dma('sp', bst[hp * 64:(hp + 1) * 64, j, :], b_s[l, h4:h4 + 1, :].partition_broadcast(64), [], ['bst'], f'bs{h4}')
                s0 = wget(('in', l, 0))
                s1 = wget(('in', l, 1))
                p.op('pool', lambda e: e.memset(vz, 0.0), [], ['vz'])
                if first:
                    p.op('pool', lambda e: e.memset(glu[:, :, 0:30], 0.0), [], [('glu', i, 'halo') for i in range(3)])
                    p.op('pool', lambda e: e.memset(prod[:, :, 0:2], 0.0), [], [('prod', i, 'halo') for i in range(3)])
                else:
                    cp('pool', glu[:, :, 0:30], gstate[:, l, :, :], [('gstate', l)], [('glu', i, 'halo') for i in range(3)])
                    cp('pool', prod[:, :, 0:2], pstate[:, l, :, :], [('pstate', l)], [('prod', i, 'halo') for i in range(3)])
                for i in range(3):
                    for k in range(31):
                        ts('pool', dg[:, i * 31 + k, :], identb[:], caw[:, i, l * 31 + k:l * 31 + k + 1], 0.0, ALU.mult, ALU.add,
                           ['identb', 'caw'], [('dg', i, k)])
                    for k in range(3):
                        ts('pool', dgc[:, i * 3 + k, :], identb[:], misc[:, i, 12 + l * 3 + k:12 + l * 3 + k + 1], 0.0, ALU.mult, ALU.add,
                           ['identb', 'misc'], [('dgc', i, k)])
                p.tag = f'glu t{ti} l{l}'
                for h in range(2):
                    for i in range(3):
                        bv = nextpf()
                        bg = nextpf()
                        for k in range(8):
                            mm(pf[bv][:, :], ring[s0][:, k, i * 128:(i + 1) * 128], xT[:, k, h * 512:(h + 1) * 512], k == 0, k == 7,
                               [('w', s0)] + xT_res(h), [('pf', bv)])
                        for k in range(8):
                            mm(pf[bg][:, :], ring[s1][:, k, i * 128:(i + 1) * 128], xT[:, k, h * 512:(h + 1) * 512], k == 0, k == 7,
                               [('w', s1)] + xT_res(h), [('pf', bg)])
                        sb_ = (h * 3 + i) % 2
                        act(sigt[:, sb_, :], pf[bg][:, :], AF.Sigmoid, [('pf', bg)], [('sigt', sb_)])
                        tt('dve', glu[:, i, 30 + h * 512:30 + (h + 1) * 512], pf[bv][:, :], sigt[:, sb_, :], ALU.mult,
                           [('pf', bv), ('sigt', sb_)], [('glu', i, h)])
                    if h == 0 and pend['x1'] is not None:
                        make_xT(1, *pend['x1'])
                        pend['x1'] = None
                        umark('B')
                cp('pool', gstate[:, l, :, :], glu[:, :, 1024:1054], [('glu', i, 1) for i in range(3)], [('gstate', l)])
                p.tag = f'rest t{ti} l{l}'
                s2 = wget(('in', l, 2))
                vzv = vz.rearrange("p (c j a b) -> p c j a b", c=4, j=2, a=4)
                vzm = vz.rearrange("p (c j n) -> p c j n", c=4, j=2)

                def b_part1(h):
                    for j in range(2):
                        b = nextpf()
                        for k in range(8):
                            mm(pf[b][:, :], ring[s2][:, k, j * 128:(j + 1) * 128], xT[:, k, h * 512:(h + 1) * 512], k == 0, k == 7,
                               [('w', s2)] + xT_res(h), [('pf', b)])
                        act(u_t[:, j, :], pf[b][:, :], AF.Gelu, [('pf', b)], [('u', j)])
                    for cc in range(4):
                        c = 4 * h + cc
                        b = nextpf()
                        for k in range(8):
                            mm(pf[b][:, 0:256], xT[:, k, c * 128:(c + 1) * 128], ring[s2][:, k, 256:512], k == 0, k == 7,
                               [('w', s2), ('xT', c)], [('pf', b)])
                        Vr = [('vg', cc)]
                        Vs = [('vst', cc)]
                        act(vg[:, cc, :], pf[b][:, 0:256], AF.Gelu, [('pf', b)], Vr)
                        p.op('dve', lambda e, cc=cc: e.bn_stats(out=vst[:, cc, :], in_=vg[:, cc, :]), Vr, Vs)
                        p.op('dve', lambda e, cc=cc: e.bn_aggr(out=vmv[:, cc, 0:2], in_=vst[:, cc, :]), Vs, Vs)
                        act(vmv[:, cc, 2:3], vmv[:, cc, 1:2], AF.Sqrt, Vs + ['epst'], Vs, bias=epst[:, 0:1])
                        p.op('dve', lambda e, cc=cc: e.reciprocal(out=vmv[:, cc, 3:4], in_=vmv[:, cc, 2:3]), Vs, Vs)
                        ts('dve', vg[:, cc, :], vg[:, cc, :], vmv[:, cc, 0:1], vmv[:, cc, 3:4], ALU.subtract, ALU.mult, Vr + Vs, Vr)
                        tt('pool', vg[:, cc, :], vg[:, cc, :], lvgb[:, 0, :], ALU.mult, Vr + ['lvgb'], Vr)
                        tt('pool', vzv[:, cc, :, 0:4:3, :], vg[:, cc, :].rearrange("p (j a b) -> p j a b", j=2, a=2),
                           lvgb[:, 1, :].rearrange("p (j a b) -> p j a b", j=2, a=2), ALU.add, Vr + ['lvgb', 'vz'], [('vzc', cc)])

                def b_part2(h):
                    for cc in range(4):
                        c = 4 * h + cc
                        b2 = nextpf()
                        for j in range(2):
                            for hp in range(2):
                                mm(pf[b2][:, j * 128:(j + 1) * 128], vzm[:, cc, j, hp * 128:(hp + 1) * 128], wsT[:, l * 4 + 2 * j + hp, :],
                                   hp == 0, hp == 1, [('vzc', cc), 'wsT'], [('pf', b2)])
                        tt('dve', tmpb, pf[b2][:, 0:256], bst[:, :, :].rearrange("p j n -> p (j n)"), ALU.add,
                           [('pf', b2), 'bst'], ['tmpb'])
                        tt('dve', cat[:, 3:5, c * 128:(c + 1) * 128], tmpb.rearrange("p (j n) -> p j n", j=2),
                           u_t[:, :, cc * 128:(cc + 1) * 128], ALU.mult, ['tmpb', ('u', 0), ('u', 1)],
                           [('cat', 3, c), ('cat', 4, c)])

                def conv_a(h):
                    for i in range(3):
                        b = nextpf()
                        rr = [('glu', i, 0)] + ([('glu', i, 'halo')] if h == 0 else [('glu', i, 1)])
                        for k in range(31):
                            mm(pf[b][:, :], dg[:, i * 31 + k, :], glu[:, i, h * 512 + k:h * 512 + k + 512], k == 0, k == 30,
                               rr + [('dg', i, k)], [('pf', b)])
                        act(a_t[:, i, :], pf[b][:, :], AF.Identity, [('pf', b), 'misc'], [('a', i)], bias=misc[:, i, l:l + 1])
                        act(asq[:, i, :], pf[b][:, :], AF.Square, [('pf', b), 'misc'], [('asq', i)], bias=misc[:, i, l:l + 1])
                    b1 = nextpf()
                    b2 = nextpf()
                    for i in range(3):
                        mm(pf[b1][:, :], onesf[:], a_t[:, i, :], i == 0, i == 2, ['onesf', ('a', i)], [('pf', b1)])
                    for i in range(3):
                        mm(pf[b2][:, :], onesf[:], asq[:, i, :], i == 0, i == 2, ['onesf', ('asq', i)], [('pf', b2)])
                    ts('dve', mu_t, pf[b1][:, :], 1.0 / 384, None, ALU.mult, None, [('pf', b1)], ['mu'])
                    tt('dve', var_t, mu_t, mu_t, ALU.mult, ['mu'], ['var'])
                    stt(var_t, pf[b2][:, :], 1.0 / 384, var_t, ALU.mult, ALU.subtract, [('pf', b2), 'var'], ['var'])
                    act(rsd_t, var_t, AF.Sqrt, ['var', 'epst'], ['rsd'], bias=epst[:, 0:1])
                    p.op('dve', lambda e: e.reciprocal(out=rsd_t, in_=rsd_t), ['rsd'], ['rsd'])

                def conv_a_tail(h):
                    for i in range(3):
                        tt('dve', a_t[:, i, :], a_t[:, i, :], mu_t, ALU.subtract, [('a', i), 'mu'], [('a', i)])
                        tt('dve', a_t[:, i, :], a_t[:, i, :], rsd_t, ALU.mult, [('a', i), 'rsd'], [('a', i)])
                        act(cat[:, i, h * 512:(h + 1) * 512], a_t[:, i, :], AF.Silu, [('a', i), 'misc'],
                            [('cat', i, 4 * h + cc) for cc in range(4)],
                            bias=misc[:, i, 8 + l:8 + l + 1], scale=misc[:, i, 4 + l:4 + l + 1])

                b_part1(0)
                flush_pdefer(4)
                conv_a(0)
                b_part2(0)
                b_part1(1)
                flush_pdefer()
                conv_a_tail(0)
                conv_a(1)
                b_part2(1)
                conv_a_tail(1)
                s4 = wget(('in', l, 4))
                s5 = wget(('in', l, 5))
                for h in range(2):
                    for i in range(3):
                        bc_ = nextpf()
                        bx = nextpf()
                        for k in range(8):
                            mm(pf[bc_][:, :], ring[s4][:, k, i * 128:(i + 1) * 128], xT[:, k, h * 512:(h + 1) * 512], k == 0, k == 7,
                               [('w', s4)] + xT_res(h), [('pf', bc_)])
                        for k in range(8):
                            mm(pf[bx][:, :], ring[s5][:, k, i * 128:(i + 1) * 128], xT[:, k, h * 512:(h + 1) * 512], k == 0, k == 7,
                               [('w', s5)] + xT_res(h), [('pf', bx)])
                        sb_ = (h * 3 + i) % 2
                        act(sigt[:, sb_, :], pf[bx][:, :], AF.Copy, [('pf', bx)], [('sigt', sb_)])
                        tt('dve', prod[:, i, 2 + h * 512:2 + (h + 1) * 512], pf[bc_][:, :], sigt[:, sb_, :], ALU.mult,
                           [('pf', bc_), ('sigt', sb_)], [('prod', i, h)])
                cp('pool', pstate[:, l, :, :], prod[:, :, 1024:1026], [('prod', i, 1) for i in range(3)], [('pstate', l)])
                s3 = wget(('in', l, 3))
                for h in range(2):
                    for i in range(3):
                        bb = nextpf()
                        bk = nextpf()
                        for k in range(8):
                            mm(pf[bb][:, :], ring[s3][:, k, i * 128:(i + 1) * 128], xT[:, k, h * 512:(h + 1) * 512], k == 0, k == 7,
                               [('w', s3)] + xT_res(h), [('pf', bb)])
                        rr = [('prod', i, 0)] + ([('prod', i, 'halo')] if h == 0 else [('prod', i, 1)])
                        for k in range(3):
                            mm(pf[bk][:, :], dgc[:, i * 3 + k, :], prod[:, i, h * 512 + k:h * 512 + k + 512], k == 0, k == 2,
                               rr + [('dgc', i, k)], [('pf', bk)])
                        sb_ = (h * 3 + i) % 2
                        act(sigt[:, sb_, :], pf[bk][:, :], AF.Copy, [('pf', bk)], [('sigt', sb_)])
                        tt('dve', cat[:, 5 + i, h * 512:(h + 1) * 512], pf[bb][:, :], sigt[:, sb_, :], ALU.mult,
                           [('pf', bb), ('sigt', sb_)], [('cat', 5 + i, 4 * h + cc) for cc in range(4)])
                umark()
                umark('B')
                load_gbt(l, 0)

                lp1 = LNPipe(True, (0 * 4 + l, 1 * 4 + l))

                def after1(hf, c):
                    if hf == 1:
                        lp1.push(c)
                proj_out('out', l, after1)
                lp1.flush()
                k_proj(l)
                make_xT(0, 0 * 4 + l, 1 * 4 + l)
                v_proj(l)

                sq = [wget(('q', l, 0)), wget(('q', l, 1))]
                evq = [0]

                def q_group(blk, j, h):
                    b = nextpf()
                    s_ = sq[blk]
                    for k in range(8):
                        mm(pf[b][:, :], ring[s_][:, k, j * 128:(j + 1) * 128], xT[:, k, h * 512:(h + 1) * 512], k == 0, k == 7,
                           [('w', s_)] + xT_res(h), [('pf', b)])
                    if evq[0] % 2 == 0:
                        act(qT[:, blk * 4 + j, h * 512:(h + 1) * 512], pf[b][:, :], AF.Copy, [('pf', b)], [('qT', blk * 4 + j, h)])
                    else:
                        cp('dve', qT[:, blk * 4 + j, h * 512:(h + 1) * 512], pf[b][:, :], [('pf', b)], [('qT', blk * 4 + j, h)])
                    evq[0] += 1
                flush_pdefer()
                for blk in range(2):
                    for j in range(4):
                        q_group(blk, j, 0)
                make_xT(1, 0 * 4 + l, 1 * 4 + l)

                def att_scores(h, cc):
                    c = 4 * h + cc
                    par = cc % 2
                    bs_ = [nextpf(), nextpf()]
                    for hd in range(4):
                        for dc in range(2):
                            mm(pf[bs_[hd // 2]][:, (hd % 2) * 256:(hd % 2 + 1) * 256], qT[:, hd * 2 + dc, c * 128:(c + 1) * 128],
                               KT[:, hd * 2 + dc, :], dc == 0, dc == 1, [('qT', hd * 2 + dc, h), 'KT'], [('pf', bs_[hd // 2])])
                    A = [('att', par)]
                    for q2 in range(2):
                        p.op('dve', lambda e, q2=q2, par=par, bq=bs_[q2]: e.tensor_reduce(
                            out=amx[:, par, q2 * 2:q2 * 2 + 2], in_=pf[bq][:, :].rearrange("p (h n) -> p h n", h=2),
                            axis=AX.X, op=ALU.max), [('pf', bs_[q2])], A)
                    ts('dve', amx[:, par, 4:8], amx[:, par, 0:4], -1.0 / 16, None, ALU.mult, None, A, A)
                    for hd in range(4):
                        act(pe_[:, par, hd, :], pf[bs_[hd // 2]][:, (hd % 2) * 256:(hd % 2 + 1) * 256], AF.Exp,
                            [('pf', bs_[hd // 2])] + A, [('pe', par, hd)] + A,
                            bias=amx[:, par, 4 + hd:5 + hd], scale=1.0 / 16, accum=asum[:, par, hd:hd + 1])
                    p.op('dve', lambda e, par=par: e.reciprocal(out=asum[:, par, 4:8], in_=asum[:, par, 0:4]), A, A)
                    for hd in range(4):
                        ts('dve', pn[:, par, hd, :], pe_[:, par, hd, :], asum[:, par, 4 + hd:5 + hd], None, ALU.mult, None,
                           [('pe', par, hd)] + A, [('pn', par)])

                def att_tr(h, cc):
                    par = cc % 2
                    b = nextpb()
                    for hd in range(4):
                        for mc in range(2):
                            tr(pb[b][:, (hd * 2 + mc) * 128:(hd * 2 + mc + 1) * 128], pn[:, par, hd, mc * 128:(mc + 1) * 128], identb[:],
                               [('pn', par), 'identb'], [('pb', b)])
                    if cc % 2 == 0:
                        act(pT[:, h, :, :, cc * 128:(cc + 1) * 128], pb[b][:, :].rearrange("p (h m n) -> p h m n", h=4, m=2), AF.Copy,
                            [('pb', b)], [('pT', h, cc)])
                    else:
                        cp('dve', pT[:, h, :, :, cc * 128:(cc + 1) * 128], pb[b][:, :].rearrange("p (h m n) -> p h m n", h=4, m=2),
                           [('pb', b)], [('pT', h, cc)])

                def pv_group(h, g8):
                    hd, dc = g8 // 2, g8 % 2
                    b = nextpf()
                    for mc in range(2):
                        mm(pf[b][:, :], Vt[:, mc, hd * 256 + dc * 128:hd * 256 + (dc + 1) * 128], pT[:, h, hd, mc, :], mc == 0, mc == 1,
                           ['Vt'] + [('pT', h, cc) for cc in range(4)], [('pf', b)])
                    wr = [('cat', hd * 2 + dc, 4 * h + cc) for cc in range(4)]
                    if dc == 0:
                        act(cat[:, hd * 2 + dc, h * 512:(h + 1) * 512], pf[b][:, :], AF.Copy, [('pf', b)], wr)
                    else:
                        cp('dve', cat[:, hd * 2 + dc, h * 512:(h + 1) * 512], pf[b][:, :], [('pf', b)], wr)

                seqc = [(h, cc) for h in range(2) for cc in range(4)]
                qfill = [(blk, j) for blk in range(2) for j in range(4)]
                qsched = {0: qfill[0:2], 1: qfill[2:4], 2: qfill[4:6], 3: qfill[6:8]}
                att_scores(*seqc[0])
                for idx in range(8):
                    h, cc = seqc[idx]
                    if h == 0:
                        for (blk, j) in qsched[cc]:
                            q_group(blk, j, 1)
                    if idx + 1 < 8:
                        att_scores(*seqc[idx + 1])
                    if h == 1:
                        pv_group(0, 2 * cc)
                        pv_group(0, 2 * cc + 1)
                    att_tr(h, cc)
                for g8 in range(8):
                    pv_group(1, g8)
                umark()
                load_gbt(l, 1)

                lp2 = LNPipe(True, (2 * 4 + l, 3 * 4 + l))

                def after2(hf, c):
                    if hf == 1:
                        lp2.push(c)
                proj_out('o', l, after2)
                lp2.flush()
                make_xT(0, 2 * 4 + l, 3 * 4 + l)


                def store_chunk(c, ti=ti, row0=row0):
                    dma('sp', out[row0 + c * 128:row0 + (c + 1) * 128, :], x_tm[:, c, :],
                        [('x', c)], [('o', ti, c)], f'xout{c}')
                lp3 = LNPipe(not lastl, None if lastl else (4 * 4 + l, 5 * 4 + l), store_chunk if lastl else None)
                ri = 0
                for g in range(2):
                    if g == 1:
                        load_gbt(l, 2)
                    for blk in range(4):
                        s = wget(('f1', l, g, blk))
                        if g == 0 and blk in (1, 2):
                            flush_pdefer(4)
                        if g == 0 and blk == 0:
                            order = [(j, 0) for j in range(4)] + ['x1'] + [(j, 1) for j in range(4)]
                        else:
                            order = [(j, h) for j in range(4) for h in range(2)]
                        for it in order:
                            if it == 'x1':
                                make_xT(1, 2 * 4 + l, 3 * 4 + l)
                                continue
                            j, h = it
                            if True:
                                b = nextpf()
                                for k in range(8):
                                    mm(pf[b][:, :], ring[s][:, k, j * 128:(j + 1) * 128], xT[:, k, h * 512:(h + 1) * 512], k == 0, k == 7,
                                       [('w', s)] + xT_res(h), [('pf', b)])
                                rb = ri % 2
                                ri += 1
                                act(rt[:, rb, :], pf[b][:, :], AF.Relu, [('pf', b)], [('rt', rb)])
                                tt('dve', hT[:, blk * 4 + j, h * 512:(h + 1) * 512], rt[:, rb, :], pf[b][:, :], ALU.mult,
                                   [('rt', rb), ('pf', b)], [('hT', blk * 4 + j, h)])
                    for cb in range(2):
                        sa = wget(('f2', l, g, cb, 0))
                        sb2 = wget(('f2', l, g, cb, 1))
                        for c in range(NCH):
                            b = nextpf()
                            for kk in range(2):
                                s = sa if kk == 0 else sb2
                                for k in range(8):
                                    mm(pf[b][:, :], hT[:, kk * 8 + k, c * 128:(c + 1) * 128], ring[s][:, k, 0:512],
                                       kk == 0 and k == 0, kk == 1 and k == 7,
                                       [('w', s), ('hT', kk * 8 + k, c // 4)], [('pf', b)])
                            xs = x_tm[:, c, cb * 512:(cb + 1) * 512]
                            if g == 0:
                                stt(xs, xs, ALPHA, pf[b][:, :], ALU.mult, ALU.add, [('x', c), ('pf', b)], [('x', c)])
                            else:
                                tt('dve', xs, xs, pf[b][:, :], ALU.add, [('x', c), ('pf', b)], [('x', c)])
                            if g == 1 and cb == 1:
                                lp3.push(c)
                lp3.flush()
                if not lastl:
                    make_xT(0, 4 * 4 + l, 5 * 4 + l)
                    pend['x1'] = (4 * 4 + l, 5 * 4 + l)
                umark()
        p.op('sp', None, [('o', ti, c) for ti in range(NT) for c in range(NCH)], [])
        assert wstate['use'] == len(specs)
        p.emit(nc, stack)
    nc._dbg = p.dbg
    return nc


WNAMES = ['w_in', 'conv_a_w', 'conv_a_b', 'ln_a_g', 'ln_a_b', 'ln_v_g', 'ln_v_b', 'w_s', 'b_s', 'conv_c_w', 'w_out',
          'ln1_g', 'ln1_b', 'w_q', 'w_kv', 'w_o', 'ln2_g', 'ln2_b', 'w_ff1', 'w_ff2', 'ln3_g', 'ln3_b']


def prep_weights(inputs):
    wd = {}
    for n in WNAMES:
        a = np.ascontiguousarray(np.asarray(inputs[n], dtype=np.float32))
        if n == 'conv_a_w':
            a = a.reshape(DEPTH * 31, 384)
        elif n == 'conv_c_w':
            a = a.reshape(DEPTH * 3, 384)
        wd[n] = a
    return wd


def kernel(**inputs):
    ncores = 8
    x = np.asarray(inputs['x'], dtype=np.float32)
    mem = np.asarray(inputs['mem'], dtype=np.float32)
    B = x.shape[0]
    nseq = B // ncores
    wd = prep_weights(inputs)
    nc = build(nseq, DEPTH)
    in_maps = []
    for c in range(ncores):
        m = dict(wd)
        m['x'] = np.ascontiguousarray(x[c * nseq:(c + 1) * nseq].reshape(nseq * S, D))
        m['mem'] = np.ascontiguousarray(mem[c * nseq:(c + 1) * nseq].reshape(nseq * MEM, D))
        in_maps.append(m)
    res = run_bass_kernel_spmd(nc, in_maps, core_ids=list(range(ncores)))
    outs = [np.asarray(r['out']).reshape(nseq, S, D) for r in res.results]
    return np.concatenate(outs, axis=0).astype(np.float32)
```

```python
import numpy as np
from contextlib import ExitStack
import concourse.bass as bass
import concourse.mybir as mybir
from concourse.bass_utils import run_bass_kernel_spmd

F32 = mybir.dt.float32
BF16 = mybir.dt.bfloat16
AF = mybir.ActivationFunctionType
ALU = mybir.AluOpType
AX = mybir.AxisListType

D = 1024
S = 2048
T = 1024
NCH = T // 128
MEM = 256
DEPTH = 4
ALPHA = (2.0 * DEPTH) ** 0.25
EPS = 1e-5
NSLOT = 4
NPF = 6
ENGS = ['pe', 'act', 'dve', 'pool', 'sp']
CENGS = ['pe', 'act', 'dve', 'pool']
CH = 16000
U_KINDS = {'glu', 'prod', 'dg', 'dgc', 'vz', 'vzc', 'u', 'vg', 'a', 'asq', 'sigt', 'tmpb', 'mu', 'var', 'rsd',
           'qT', 'pT', 'pe', 'pn', 'KT', 'Vt', 'hT', 'rt', 'memf', 'memb', 'wsf', 'wsb', 'R1', 'R2', 'R3'}
ZONES = {'A': U_KINDS, 'B': {'nbh', 'a', 'asq'}}


class Prog:
    def __init__(self):
        self.q = {e: [] for e in ENGS}
        self.cnt = {e: 0 for e in ENGS}
        self.lastw = {}
        self.readers = {}
        self.seen = {e: {} for e in ENGS}
        self.dma_cnt = {}
        self.u_users = {z: {} for z in ZONES}
        self.u_deps = {z: {} for z in ZONES}
        self.dbg = []
        self.tag = ''

    def umark(self, z='A'):
        for k, v in self.u_users[z].items():
            if self.u_deps[z].get(k, 0) < v:
                self.u_deps[z][k] = v
        self.u_users[z] = {}

    def op(self, eng, fn, reads=(), writes=(), stream=None):
        kinds = set((r[0] if isinstance(r, tuple) else r) for r in list(reads) + list(writes))
        zt = [z for z in ZONES if kinds & ZONES[z]]
        deps = {}
        for z in zt:
            for k, v in self.u_deps[z].items():
                if deps.get(k, 0) < v:
                    deps[k] = v
        for r in reads:
            t = self.lastw.get(r)
            if t is not None and deps.get(t[0], 0) < t[1]:
                deps[t[0]] = t[1]
        for r in writes:
            t = self.lastw.get(r)
            if t is not None and deps.get(t[0], 0) < t[1]:
                deps[t[0]] = t[1]
            for k, v in self.readers.get(r, {}).items():
                if deps.get(k, 0) < v:
                    deps[k] = v
        if fn is None:
            me = None
        elif stream is None:
            self.cnt[eng] += 1
            me = (eng, self.cnt[eng])
        else:
            key = ('dma', stream)
            self.dma_cnt[key] = self.dma_cnt.get(key, 0) + 1
            me = (key, self.dma_cnt[key])
        waits = []
        seen = self.seen[eng]
        for k, v in deps.items():
            if k == 'pe' and eng == 'pe':
                continue
            if seen.get(k, 0) >= v:
                continue
            seen[k] = v
            waits.append((k, v))
        self.q[eng].append((waits, fn, me))
        self.dbg.append((eng, self.tag, me, tuple(waits)))
        if me is not None:
            for z in zt:
                if self.u_users[z].get(me[0], 0) < me[1]:
                    self.u_users[z][me[0]] = me[1]
        if me is not None:
            for r in reads:
                d = self.readers.setdefault(r, {})
                if d.get(me[0], 0) < me[1]:
                    d[me[0]] = me[1]
            for r in writes:
                self.lastw[r] = me
                self.readers[r] = {}
        return me

    def barrier(self):
        snap = {e: self.cnt[e] for e in CENGS}
        for e in ENGS:
            waits = []
            for k, v in snap.items():
                if v == 0 or self.seen[e].get(k, 0) >= v:
                    continue
                self.seen[e][k] = v
                waits.append((k, v))
            if waits:
                self.q[e].append((waits, None, None))

    def emit(self, nc, stack):
        needed = set()
        for e in ENGS:
            for waits, fn, me in self.q[e]:
                for w in waits:
                    needed.add(w)
        sig = {}
        nsig = {}
        for e in ENGS:
            n = 0
            for waits, fn, me in self.q[e]:
                if me is not None and me[0] == e and me in needed:
                    n += 1
                    sig[me] = n
            nsig[e] = n
        sems = {}
        for e in ENGS:
            sems[e] = [stack.enter_context(nc.semaphore(f"s_{e}_{i}")) for i in range((nsig[e] + CH - 1) // CH)]
        dsems = {k: stack.enter_context(nc.semaphore("d_" + str(k[1]))) for k in self.dma_cnt}

        def semval(k, v):
            if isinstance(k, tuple):
                return dsems[k], 16 * v
            idx = sig[(k, v)]
            return sems[k][(idx - 1) // CH], (idx - 1) % CH + 1

        block = stack.enter_context(nc.Block())
        q = self.q

        def run(engname):
            def body(eng):
                for waits, fn, me in q[engname]:
                    for k, v in waits:
                        s, val = semval(k, v)
                        eng.wait_ge(s, val)
                    if fn is None:
                        continue
                    ins = fn(eng)
                    if isinstance(me[0], tuple):
                        ins.then_inc(dsems[me[0]], 16)
                    elif me in sig:
                        s, val = semval(*me)
                        ins.then_inc(s, 1)
            return body
        block.tensor(run('pe'))
        block.scalar(run('act'))
        block.vector(run('dve'))
        block.gpsimd(run('pool'))
        block.sync(run('sp'))


def weight_specs(nl):
    sp = []
    for l in range(nl):
        for b in (0, 1, 2, 4, 5, 3):
            sp.append(('in', l, b))
        for hf in range(2):
            sp.append(('out', l, hf))
        for b in range(2):
            sp.append(('k', l, b))
        for b in range(2):
            sp.append(('v', l, b))
        for b in range(2):
            sp.append(('q', l, b))
        for hf in range(2):
            sp.append(('o', l, hf))
        for g in range(2):
            for b in range(4):
                sp.append(('f1', l, g, b))
            for cb in range(2):
                for kk in range(2):
                    sp.append(('f2', l, g, cb, kk))
    return sp


IN_COLS = {0: (0, 384), 1: (384, 384), 2: (768, 512), 3: (1280, 384), 4: (1664, 384), 5: (2048, 384)}


def build(nseq, nl, S=S):
    nc = bass.Bass("TRN2", target_bir_lowering=False)
    NT = nseq * S // T
    TPS = S // T

    def din(name, shape):
        return nc.dram_tensor(name, shape, F32, kind="ExternalInput").ap()
    x = din("x", [nseq * S, D])
    mem = din("mem", [nseq * MEM, D])
    w_in = din("w_in", [DEPTH, D, 2432])
    conv_a_w = din("conv_a_w", [DEPTH * 31, 384])
    conv_a_b = din("conv_a_b", [DEPTH, 384])
    ln_a_g = din("ln_a_g", [DEPTH, 384])
    ln_a_b = din("ln_a_b", [DEPTH, 384])
    ln_v_g = din("ln_v_g", [DEPTH, 256])
    ln_v_b = din("ln_v_b", [DEPTH, 256])
    w_s = din("w_s", [DEPTH, 4, 128, 128])
    b_s = din("b_s", [DEPTH, 4, 128])
    conv_c_w = din("conv_c_w", [DEPTH * 3, 384])
    w_out = din("w_out", [DEPTH, D, D])
    lng = [din("ln1_g", [DEPTH, D]), din("ln1_b", [DEPTH, D]), din("ln2_g", [DEPTH, D]),
           din("ln2_b", [DEPTH, D]), din("ln3_g", [DEPTH, D]), din("ln3_b", [DEPTH, D])]
    w_q = din("w_q", [DEPTH, D, D])
    w_kv = din("w_kv", [DEPTH, D, 2 * D])
    w_o = din("w_o", [DEPTH, D, D])
    w_ff1 = din("w_ff1", [DEPTH, D, 4 * D])
    w_ff2 = din("w_ff2", [DEPTH, 4 * D, D])
    out = nc.dram_tensor("out", [nseq * S, D], F32, kind="ExternalOutput").ap()

    stack = ExitStack()
    with stack:
        def sb(name, shape, dt):
            return stack.enter_context(nc.sbuf_tensor(name, shape, dt))
        x_tm = sb("x_tm", [128, NCH, D], F32)
        xT = sb("xT", [128, 8, T], BF16)
        nb = sb("nb", [128, 4, D], BF16)
        cat = sb("cat", [128, 8, T], BF16)
        ring = [sb(f"ring{i}", [128, 8, 512], BF16) for i in range(NSLOT)]
        gbt = sb("gbt", [128, 2, D], F32)
        memT = sb("memT", [128, 8, MEM], BF16)
        wsT = sb("wsT", [128, 16, 128], BF16)
        gbT = sb("gbT", [128, 8, 24], F32)
        caw = sb("caw", [128, 3, 124], F32)
        misc = sb("misc", [128, 3, 24], F32)
        lvgb = sb("lvgb", [128, 2, 256], F32)
        bst = sb("bst", [128, 2, 128], F32)
        identb = sb("identb", [128, 128], BF16)
        identf = sb("identf", [128, 128], F32)
        onesf = sb("onesf", [128, 128], F32)
        maskc = sb("maskc", [128, 128], F32)
        epst = sb("epst", [128, 1], F32)
        gstate = sb("gstate", [128, DEPTH, 3, 30], BF16)
        pstate = sb("pstate", [128, DEPTH, 3, 2], BF16)
        lnst = sb("lnst", [128, NCH, 12], F32)
        lnmv = sb("lnmv", [128, NCH, 2], F32)
        lnsd = sb("lnsd", [128, NCH, 3], F32)
        vst = sb("vst", [128, 4, 6], F32)
        vmv = sb("vmv", [128, 4, 4], F32)
        amx = sb("amx", [128, 2, 12], F32)
        asum = sb("asum", [128, 2, 8], F32)
        scr = sb("scr", [128, 2], F32)
        UW = 37248
        U = sb("U", [128, UW], BF16)
        pf = [stack.enter_context(nc.psum_tensor(f"pf{i}", [128, 512], F32)) for i in range(NPF)]
        pb = [stack.enter_context(nc.psum_tensor(f"pb{i}", [128, 1024], BF16)) for i in range(2)]

        def ub(off, n):
            return U[:, off:off + n]

        def uf(off, n):
            return U[:, off:off + 2 * n].bitcast(F32)
        o = 0
        glu = ub(o, 3 * 1056).rearrange("p (i n) -> p i n", i=3); o += 3 * 1056
        prod = ub(o, 3 * 1028).rearrange("p (i n) -> p i n", i=3); o += 3 * 1028
        dg = ub(o, 93 * 128).rearrange("p (k n) -> p k n", k=93); o += 93 * 128
        dgc = ub(o, 9 * 128).rearrange("p (k n) -> p k n", k=9); o += 9 * 128
        vz = ub(o, 2048); o += 2048
        u_t = uf(o, 1024).rearrange("p (j n) -> p j n", j=2); o += 2048
        vg = uf(o, 1024).rearrange("p (c n) -> p c n", c=4); o += 2048
        a_t = uf(o, 1536).rearrange("p (i n) -> p i n", i=3); o += 3072
        asq = uf(o, 1536).rearrange("p (i n) -> p i n", i=3); o += 3072
        sigt = uf(o, 1024).rearrange("p (b n) -> p b n", b=2); o += 2048
        tmpb = uf(o, 256); o += 512
        mu_t = uf(o, 512); o += 1024
        var_t = uf(o, 512); o += 1024
        rsd_t = uf(o, 512); o += 1024
        assert o <= UW, o
        o = 0
        qT = ub(o, 8 * T).rearrange("p (f n) -> p f n", f=8); o += 8 * T
        pT = ub(o, 8192).rearrange("p (g h m n) -> p g h m n", g=2, h=4, m=2); o += 8192
        pe_ = uf(o, 2048).rearrange("p (b h n) -> p b h n", b=2, h=4); o += 4096
        pn = ub(o, 2048).rearrange("p (b h n) -> p b h n", b=2, h=4); o += 2048
        KT = ub(o, 2048).rearrange("p (f n) -> p f n", f=8); o += 2048
        Vt = ub(o, 2048).rearrange("p (m n) -> p m n", m=2); o += 2048
        assert o <= 26624
        nbh = ub(26624, 4096).rearrange("p (c n) -> p c n", c=4)
        o = 0
        hT = ub(o, 16 * T).rearrange("p (f n) -> p f n", f=16); o += 16 * T
        rt = uf(o, 1024).rearrange("p (b n) -> p b n", b=2); o += 2048
        assert o <= UW
        memf = uf(0, 2048).rearrange("p (m n) -> p m n", m=2)
        memb = ub(4096, 2048).rearrange("p (m n) -> p m n", m=2)
        wsf = uf(0, 2048).rearrange("p (k n) -> p k n", k=16)
        wsb = ub(4096, 2048).rearrange("p (k n) -> p k n", k=16)
        R1 = uf(8192, 1024)
        R2 = uf(8192 + 2048, 384)
        R3 = uf(8192 + 2048 + 768, 384)

        p = Prog()
        pfi = [0]

        def nextpf():
            b = pfi[0]
            pfi[0] = (b + 1) % NPF
            return b
        pbi = [0]

        def nextpb():
            b = pbi[0]
            pbi[0] = (b + 1) % 2
            return b

        def mm(o_, lhsT, rhs, start, stop, reads, writes):
            p.op('pe', lambda e: e.matmul(o_, lhsT=lhsT, rhs=rhs, start=start, stop=stop), reads, writes)

        def tr(o_, in_, ident, reads, writes):
            p.op('pe', lambda e: e.transpose(out=o_, in_=in_, identity=ident), reads, writes)

        def act(o_, in_, func, reads, writes, bias=None, scale=None, accum=None):
            kw = {}
            if bias is not None:
                kw['bias'] = bias
            if scale is not None:
                kw['scale'] = scale
            if accum is not None:
                kw['accum_out'] = accum
            p.op('act', lambda e: e.activation(out=o_, in_=in_, func=func, **kw), reads, writes)

        def tt(eng, o_, in0, in1, op, reads, writes):
            p.op(eng, lambda e: e.tensor_tensor(out=o_, in0=in0, in1=in1, op=op), reads, writes)

        def ts(eng, o_, in0, s1, s2, op0, op1, reads, writes):
            if op1 is None:
                p.op(eng, lambda e: e.tensor_scalar(out=o_, in0=in0, scalar1=s1, scalar2=None, op0=op0), reads, writes)
            else:
                p.op(eng, lambda e: e.tensor_scalar(out=o_, in0=in0, scalar1=s1, scalar2=s2, op0=op0, op1=op1), reads, writes)

        def stt(o_, in0, sc, in1, op0, op1, reads, writes):
            p.op('dve', lambda e: e.scalar_tensor_tensor(out=o_, in0=in0, scalar=sc, in1=in1, op0=op0, op1=op1), reads, writes)

        def cp(eng, o_, in_, reads, writes):
            p.op(eng, lambda e: e.tensor_copy(out=o_, in_=in_), reads, writes)

        def dma(eng, o_, in_, reads, writes, stream):
            p.op(eng, lambda e: e.dma_start(out=o_, in_=in_), reads, writes, stream=stream)

        specs = []
        for ti in range(NT):
            specs += weight_specs(nl)
        wstate = {'use': 0, 'iss': 0}

        def wsrc(key):
            kind, l = key[0], key[1]
            if kind == 'in':
                c0, n = IN_COLS[key[2]]
                return w_in[l, :, c0:c0 + n], n
            if kind == 'out':
                return w_out[l, :, key[2] * 512:(key[2] + 1) * 512], 512
            if kind == 'k':
                return w_kv[l, :, key[2] * 512:(key[2] + 1) * 512], 512
            if kind == 'v':
                return w_kv[l, :, 1024 + key[2] * 512:1024 + (key[2] + 1) * 512], 512
            if kind == 'q':
                return w_q[l, :, key[2] * 512:(key[2] + 1) * 512], 512
            if kind == 'o':
                return w_o[l, :, key[2] * 512:(key[2] + 1) * 512], 512
            if kind == 'f1':
                c0 = key[2] * 2048 + key[3] * 512
                return w_ff1[l, :, c0:c0 + 512], 512
            if kind == 'f2':
                r0 = key[2] * 2048 + key[4] * 1024
                return w_ff2[l, r0:r0 + 1024, key[3] * 512:(key[3] + 1) * 512], 512
            raise ValueError(key)

        def wissue(j):
            src, n = wsrc(specs[j])
            slot = j % NSLOT
            dma('pool', ring[slot][:, :, 0:n], src.rearrange("(k p) n -> p k n", p=128), [], [('w', slot)], f"w{slot}")

        def wget(key):
            i = wstate['use']
            assert specs[i] == key, (specs[i], key)
            lim = min(len(specs), i + NSLOT - 1)
            while wstate['iss'] < lim:
                wissue(wstate['iss'])
                wstate['iss'] += 1
            wstate['use'] += 1
            return i % NSLOT

        p.op('pool', lambda e: e.memset(identb[:], 1.0), [], ['identb'])
        p.op('pool', lambda e: e.affine_select(out=identb[:], in_=identb[:], pattern=[[-1, 128]], compare_op=ALU.is_equal,
                                                fill=0.0, base=0, channel_multiplier=1), ['identb'], ['identb'])
        p.op('pool', lambda e: e.memset(identf[:], 1.0), [], ['identf'])
        p.op('pool', lambda e: e.affine_select(out=identf[:], in_=identf[:], pattern=[[-1, 128]], compare_op=ALU.is_equal,
                                                fill=0.0, base=0, channel_multiplier=1), ['identf'], ['identf'])
        p.op('pool', lambda e: e.memset(onesf[:], 1.0), [], ['onesf'])
        p.op('pool', lambda e: e.memset(epst[:], EPS), [], ['epst'])
        p.op('pool', lambda e: e.memset(maskc[:], 1.0), [], ['maskc'])
        p.op('pool', lambda e: e.affine_select(out=maskc[:], in_=maskc[:], pattern=[[1, 128]], compare_op=ALU.is_ge,
                                                fill=0.0, base=0, channel_multiplier=-1), ['maskc'], ['maskc'])
        for i6 in range(6):
            dma('sp', R1[i6 * 4:(i6 + 1) * 4, :], lng[i6], [], [('R1', i6)], f'parR1_{i6}')
        dma('sp', R2[0:124, :], conv_a_w, [], ['R2'], 'parR2')
        dma('sp', R3[0:4, :], conv_a_b, [], [('R3', 0)], 'parR30')
        dma('sp', R3[4:8, :], ln_a_g, [], [('R3', 1)], 'parR31')
        dma('sp', R3[8:12, :], ln_a_b, [], [('R3', 2)], 'parR32')
        dma('sp', R3[12:24, :], conv_c_w, [], [('R3', 3)], 'parR33')
        dma('sp', wsf, w_s.rearrange("l h t s -> t (l h) s"), [], ['wsf'], 'parws')
        for f in range(8):
            b = nextpf()
            tr(pf[b][:, 0:24], R1[0:24, f * 128:(f + 1) * 128], identf[0:24, 0:24],
               [('R1', i6) for i6 in range(6)] + ['identf'], [('pf', b)])
            cp('dve', gbT[:, f, :], pf[b][:, 0:24], [('pf', b)], ['gbT'])
        for i in range(3):
            b = nextpf()
            tr(pf[b][:, 0:124], R2[0:124, i * 128:(i + 1) * 128], identf[0:124, 0:124], ['R2', 'identf'], [('pf', b)])
            cp('dve', caw[:, i, :], pf[b][:, 0:124], [('pf', b)], ['caw'])
            b = nextpf()
            tr(pf[b][:, 0:24], R3[0:24, i * 128:(i + 1) * 128], identf[0:24, 0:24],
               [('R3', k) for k in range(4)] + ['identf'], [('pf', b)])
            cp('dve', misc[:, i, :], pf[b][:, 0:24], [('pf', b)], ['misc'])
        cp('dve', wsb, wsf, ['wsf'], ['wsb'])
        for q8 in range(2):
            b = nextpb()
            for k in range(8):
                tr(pb[b][:, k * 128:(k + 1) * 128], wsb[:, q8 * 8 + k, :], identb[:], ['wsb', 'identb'], [('pb', b)])
            for k in range(8):
                tt('dve', wsT[:, q8 * 8 + k, :], pb[b][:, k * 128:(k + 1) * 128], maskc[:], ALU.mult,
                   [('pb', b), 'maskc'], ['wsT'])

        def umark(z='A'):
            p.umark(z)

        def load_gbt(l, n):
            dma('sp', gbt[:, 0, :], lng[2 * n][l:l + 1, :].partition_broadcast(128), [], [('gbt', 0)], 'gbt0')
            dma('sp', gbt[:, 1, :], lng[2 * n + 1][l:l + 1, :].partition_broadcast(128), [], [('gbt', 1)], 'gbt1')

        def nbs(c):
            return nb[:, c, :] if c < 4 else nbh[:, c - 4, :]

        def nbr(c):
            return ('nb', c) if c < 4 else ('nbh', c)

        def make_xT(h, gcol, bcol):
            for fp in range(4):
                b = nextpb()
                for f in (2 * fp, 2 * fp + 1):
                    off = (f % 2) * 512
                    for cc in range(4):
                        c = 4 * h + cc
                        tr(pb[b][:, off + cc * 128:off + (cc + 1) * 128], nbs(c)[:, f * 128:(f + 1) * 128], identb[:],
                           [nbr(c), 'identb'], [('pb', b)])
                for f in (2 * fp, 2 * fp + 1):
                    off = (f % 2) * 512
                    wr = [('xT', 4 * h + cc) for cc in range(4)]
                    if gcol is None:
                        act(xT[:, f, h * 512:(h + 1) * 512], pb[b][:, off:off + 512], AF.Copy, [('pb', b)], wr)
                    else:
                        act(xT[:, f, h * 512:(h + 1) * 512], pb[b][:, off:off + 512], AF.Identity,
                            [('pb', b), 'gbT'], wr,
                            bias=gbT[:, f, bcol:bcol + 1], scale=gbT[:, f, gcol:gcol + 1])

        def ln_s1(c):
            R = [('x', c)]
            L = [('lnst', c)]
            p.op('dve', lambda e: e.bn_stats(out=lnst[:, c, 0:6], in_=x_tm[:, c, 0:512]), R, L)
            p.op('dve', lambda e: e.bn_stats(out=lnst[:, c, 6:12], in_=x_tm[:, c, 512:1024]), R, L)
            p.op('dve', lambda e: e.bn_aggr(out=lnmv[:, c, :], in_=lnst[:, c, :]), L, L)
            act(lnsd[:, c, 0:1], lnmv[:, c, 1:2], AF.Sqrt, L + ['epst'], L, bias=epst[:, 0:1])

        def ln_s2(c, want_nb):
            R = [('x', c)]
            L = [('lnst', c)]
            p.op('dve', lambda e: e.reciprocal(out=lnsd[:, c, 1:2], in_=lnsd[:, c, 0:1]), L, L)
            ts('dve', lnsd[:, c, 2:3], lnmv[:, c, 0:1], -1.0, lnsd[:, c, 1:2], ALU.mult, ALU.mult, L, L)
            if want_nb:
                act(nbs(c), x_tm[:, c, :], AF.Identity, R + L, [nbr(c)],
                    bias=lnsd[:, c, 2:3], scale=lnsd[:, c, 1:2])

        pdefer = []

        def flush_pdefer(n=None):
            k = len(pdefer) if n is None else min(n, len(pdefer))
            for _ in range(k):
                pdefer.pop(0)()

        def ln_s3(c, fast=False):
            xs = x_tm[:, c, :]
            R = [('x', c)]
            L = [('lnst', c)]
            if fast:
                act(xs, xs, AF.Identity, R + L, R, bias=lnsd[:, c, 2:3], scale=lnsd[:, c, 1:2])
                tt('dve', xs, xs, gbt[:, 0, :], ALU.mult, R + [('gbt', 0)], R)
                tt('pool', xs, xs, gbt[:, 1, :], ALU.add, R + [('gbt', 1)], R)
            else:
                def later():
                    ts('pool', xs, xs, lnsd[:, c, 1:2], lnsd[:, c, 2:3], ALU.mult, ALU.add, R + L, R)
                    tt('pool', xs, xs, gbt[:, 0, :], ALU.mult, R + [('gbt', 0)], R)
                    tt('pool', xs, xs, gbt[:, 1, :], ALU.add, R + [('gbt', 1)], R)
                pdefer.append(later)

        class LNPipe:
            def __init__(self, want_nb, xt_args, done=None):
                self.want_nb = want_nb
                self.xt_args = xt_args
                self.done = done

            def _s3(self, c):
                ln_s3(c, fast=(self.done is not None))
                if self.done is not None:
                    self.done(c)

            def push(self, c):
                ln_s1(c)
                if c >= 1:
                    ln_s2(c - 1, self.want_nb)
                if c >= 2:
                    self._s3(c - 2)

            def flush(self):
                ln_s2(NCH - 1, self.want_nb)
                self._s3(NCH - 2)
                self._s3(NCH - 1)

        def xT_res(h):
            return [('xT', 4 * h + cc) for cc in range(4)]

        def cat_res(c):
            return [('cat', f, c) for f in range(8)]

        def proj_out(kind, l, after):
            for hf in range(2):
                s = wget((kind, l, hf))
                for c in range(NCH):
                    b = nextpf()
                    for k in range(8):
                        mm(pf[b][:, :], cat[:, k, c * 128:(c + 1) * 128], ring[s][:, k, 0:512], k == 0, k == 7,
                           [('w', s)] + cat_res(c), [('pf', b)])
                    xs = x_tm[:, c, hf * 512:(hf + 1) * 512]
                    stt(xs, xs, ALPHA, pf[b][:, :], ALU.mult, ALU.add, [('x', c), ('pf', b)], [('x', c)])
                    after(hf, c)

        def k_proj(l):
            for blk in range(2):
                s = wget(('k', l, blk))
                for j in range(4):
                    b = nextpf()
                    for k in range(8):
                        mm(pf[b][:, 0:256], ring[s][:, k, j * 128:(j + 1) * 128], memT[:, k, :], k == 0, k == 7,
                           [('w', s), 'memT'], [('pf', b)])
                    if j % 2 == 0:
                        act(KT[:, blk * 4 + j, :], pf[b][:, 0:256], AF.Copy, [('pf', b)], ['KT'])
                    else:
                        cp('dve', KT[:, blk * 4 + j, :], pf[b][:, 0:256], [('pf', b)], ['KT'])

        def v_proj(l):
            for blk in range(2):
                s = wget(('v', l, blk))
                for mc in range(2):
                    b = nextpf()
                    for k in range(8):
                        mm(pf[b][:, :], memT[:, k, mc * 128:(mc + 1) * 128], ring[s][:, k, 0:512], k == 0, k == 7,
                           [('w', s), 'memT'], [('pf', b)])
                    if mc == 0:
                        cp('dve', Vt[:, mc, blk * 512:(blk + 1) * 512], pf[b][:, :], [('pf', b)], ['Vt'])
                    else:
                        act(Vt[:, mc, blk * 512:(blk + 1) * 512], pf[b][:, :], AF.Copy, [('pf', b)], ['Vt'])

        umark()
        for ti in range(NT):
            seq = ti // TPS
            first = (ti % TPS == 0)
            row0 = ti * T
            if ti == 0:
                for c in range(NCH):
                    dma('sp', x_tm[:, c, :], x[row0 + c * 128:row0 + (c + 1) * 128, :], [], [('x', c)], f'xin{c}')
            if first:
                dma('sp', memf, mem[seq * MEM:(seq + 1) * MEM, :].rearrange("(m p) d -> p m d", p=128), [], ['memf'], 'mem')
                cp('dve', memb, memf, ['memf'], ['memb'])
                for q4 in range(2):
                    b = nextpb()
                    for fq in range(4):
                        for mc in range(2):
                            f = q4 * 4 + fq
                            tr(pb[b][:, (fq * 2 + mc) * 128:(fq * 2 + mc + 1) * 128], memb[:, mc, f * 128:(f + 1) * 128],
                               identb[:], ['memb', 'identb'], [('pb', b)])
                    act(memT[:, q4 * 4:(q4 + 1) * 4, :].rearrange("p f n -> p (f n)"), pb[b][:, :], AF.Copy,
                        [('pb', b)], ['memT'])
                umark()
            for h in range(2):
                for cc in range(4):
                    c = 4 * h + cc
                    if cc % 2 == 0:
                        cp('dve', nbs(c), x_tm[:, c, :], [('x', c)], [nbr(c)])
                    else:
                        act(nbs(c), x_tm[:, c, :], AF.Copy, [('x', c)], [nbr(c)])
            make_xT(0, None, None)
            pend = {'x1': (None, None)}

            for l in range(nl):
                lastl = (l == nl - 1)
                dma('sp', lvgb[:, 0, :], ln_v_g[l:l + 1, :].partition_broadcast(128), [], ['lvgb'], 'lv0')
                dma('sp', lvgb[:, 1, :], ln_v_b[l:l + 1, :].partition_broadcast(128), [], ['lvgb'], 'lv1')
                for h4 in range(4):
                    hp, j = h4 % 2, h4 // 2
                    dma('sp', bst[hp * 64:(hp + 1) * 64, j, :], b_s[l, h4:h4 + 1, :].partition_broadcast(64), [], ['bst'], f'bs{h4}')
                s0 = wget(('in', l, 0))
                s1 = wget(('in', l, 1))
                p.op('pool', lambda e: e.memset(vz, 0.0), [], ['vz'])
                if first:
                    p.op('pool', lambda e: e.memset(glu[:, :, 0:30], 0.0), [], [('glu', i, 'halo') for i in range(3)])
                    p.op('pool', lambda e: e.memset(prod[:, :, 0:2], 0.0), [], [('prod', i, 'halo') for i in range(3)])
                else:
                    cp('pool', glu[:, :, 0:30], gstate[:, l, :, :], [('gstate', l)], [('glu', i, 'halo') for i in range(3)])
                    cp('pool', prod[:, :, 0:2], pstate[:, l, :, :], [('pstate', l)], [('prod', i, 'halo') for i in range(3)])
                for i in range(3):
                    for k in range(31):
                        ts('pool', dg[:, i * 31 + k, :], identb[:], caw[:, i, l * 31 + k:l * 31 + k + 1], 0.0, ALU.mult, ALU.add,
                           ['identb', 'caw'], [('dg', i, k)])
                    for k in range(3):
                        ts('pool', dgc[:, i * 3 + k, :], identb[:], misc[:, i, 12 + l * 3 + k:12 + l * 3 + k + 1], 0.0, ALU.mult, ALU.add,
                           ['identb', 'misc'], [('dgc', i, k)])
                p.tag = f'glu t{ti} l{l}'
                for h in range(2):
                    for i in range(3):
                        bv = nextpf()
                        bg = nextpf()
                        for k in range(8):
                            mm(pf[bv][:, :], ring[s0][:, k, i * 128:(i + 1) * 128], xT[:, k, h * 512:(h + 1) * 512], k == 0, k == 7,
                               [('w', s0)] + xT_res(h), [('pf', bv)])
                        for k in range(8):
                            mm(pf[bg][:, :], ring[s1][:, k, i * 128:(i + 1) * 128], xT[:, k, h * 512:(h + 1) * 512], k == 0, k == 7,
                               [('w', s1)] + xT_res(h), [('pf', bg)])
                        sb_ = (h * 3 + i) % 2
                        act(sigt[:, sb_, :], pf[bg][:, :], AF.Sigmoid, [('pf', bg)], [('sigt', sb_)])
                        tt('dve', glu[:, i, 30 + h * 512:30 + (h + 1) * 512], pf[bv][:, :], sigt[:, sb_, :], ALU.mult,
                           [('pf', bv), ('sigt', sb_)], [('glu', i, h)])
                    if h == 0 and pend['x1'] is not None:
                        make_xT(1, *pend['x1'])
                        pend['x1'] = None
                        umark('B')
                cp('pool', gstate[:, l, :, :], glu[:, :, 1024:1054], [('glu', i, 1) for i in range(3)], [('gstate', l)])
                p.tag = f'rest t{ti} l{l}'
                s2 = wget(('in', l, 2))
                vzv = vz.rearrange("p (c j a b) -> p c j a b", c=4, j=2, a=4)
                vzm = vz.rearrange("p (c j n) -> p c j n", c=4, j=2)

                def b_part1(h):
                    for j in range(2):
                        b = nextpf()
                        for k in range(8):
                            mm(pf[b][:, :], ring[s2][:, k, j * 128:(j + 1) * 128], xT[:, k, h * 512:(h + 1) * 512], k == 0, k == 7,
                               [('w', s2)] + xT_res(h), [('pf', b)])
                        act(u_t[:, j, :], pf[b][:, :], AF.Gelu, [('pf', b)], [('u', j)])
                    for cc in range(4):
                        c = 4 * h + cc
                        b = nextpf()
                        for k in range(8):
                            mm(pf[b][:, 0:256], xT[:, k, c * 128:(c + 1) * 128], ring[s2][:, k, 256:512], k == 0, k == 7,
                               [('w', s2), ('xT', c)], [('pf', b)])
                        Vr = [('vg', cc)]
                        Vs = [('vst', cc)]
                        act(vg[:, cc, :], pf[b][:, 0:256], AF.Gelu, [('pf', b)], Vr)
                        p.op('dve', lambda e, cc=cc: e.bn_stats(out=vst[:, cc, :], in_=vg[:, cc, :]), Vr, Vs)
                        p.op('dve', lambda e, cc=cc: e.bn_aggr(out=vmv[:, cc, 0:2], in_=vst[:, cc, :]), Vs, Vs)
                        act(vmv[:, cc, 2:3], vmv[:, cc, 1:2], AF.Sqrt, Vs + ['epst'], Vs, bias=epst[:, 0:1])
                        p.op('dve', lambda e, cc=cc: e.reciprocal(out=vmv[:, cc, 3:4], in_=vmv[:, cc, 2:3]), Vs, Vs)
                        ts('dve', vg[:, cc, :], vg[:, cc, :], vmv[:, cc, 0:1], vmv[:, cc, 3:4], ALU.subtract, ALU.mult, Vr + Vs, Vr)
                        tt('pool', vg[:, cc, :], vg[:, cc, :], lvgb[:, 0, :], ALU.mult, Vr + ['lvgb'], Vr)
                        tt('pool', vzv[:, cc, :, 0:4:3, :], vg[:, cc, :].rearrange("p (j a b) -> p j a b", j=2, a=2),
                           lvgb[:, 1, :].rearrange("p (j a b) -> p j a b", j=2, a=2), ALU.add, Vr + ['lvgb', 'vz'], [('vzc', cc)])

                def b_part2(h):
                    for cc in range(4):
                        c = 4 * h + cc
                        b2 = nextpf()
                        for j in range(2):
                            for hp in range(2):
                                mm(pf[b2][:, j * 128:(j + 1) * 128], vzm[:, cc, j, hp * 128:(hp + 1) * 128], wsT[:, l * 4 + 2 * j + hp, :],
                                   hp == 0, hp == 1, [('vzc', cc), 'wsT'], [('pf', b2)])
                        tt('dve', tmpb, pf[b2][:, 0:256], bst[:, :, :].rearrange("p j n -> p (j n)"), ALU.add,
                           [('pf', b2), 'bst'], ['tmpb'])
                        tt('dve', cat[:, 3:5, c * 128:(c + 1) * 128], tmpb.rearrange("p (j n) -> p j n", j=2),
                           u_t[:, :, cc * 128:(cc + 1) * 128], ALU.mult, ['tmpb', ('u', 0), ('u', 1)],
                           [('cat', 3, c), ('cat', 4, c)])

                def conv_a(h):
                    for i in range(3):
                        b = nextpf()
                        rr = [('glu', i, 0)] + ([('glu', i, 'halo')] if h == 0 else [('glu', i, 1)])
                        for k in range(31):
                            mm(pf[b][:, :], dg[:, i * 31 + k, :], glu[:, i, h * 512 + k:h * 512 + k + 512], k == 0, k == 30,
                               rr + [('dg', i, k)], [('pf', b)])
                        act(a_t[:, i, :], pf[b][:, :], AF.Identity, [('pf', b), 'misc'], [('a', i)], bias=misc[:, i, l:l + 1])
                        act(asq[:, i, :], pf[b][:, :], AF.Square, [('pf', b), 'misc'], [('asq', i)], bias=misc[:, i, l:l + 1])
                    b1 = nextpf()
                    b2 = nextpf()
                    for i in range(3):
                        mm(pf[b1][:, :], onesf[:], a_t[:, i, :], i == 0, i == 2, ['onesf', ('a', i)], [('pf', b1)])
                    for i in range(3):
                        mm(pf[b2][:, :], onesf[:], asq[:, i, :], i == 0, i == 2, ['onesf', ('asq', i)], [('pf', b2)])
                    ts('dve', mu_t, pf[b1][:, :], 1.0 / 384, None, ALU.mult, None, [('pf', b1)], ['mu'])
                    tt('dve', var_t, mu_t, mu_t, ALU.mult, ['mu'], ['var'])
                    stt(var_t, pf[b2][:, :], 1.0 / 384, var_t, ALU.mult, ALU.subtract, [('pf', b2), 'var'], ['var'])
                    act(rsd_t, var_t, AF.Sqrt, ['var', 'epst'], ['rsd'], bias=epst[:, 0:1])
                    p.op('dve', lambda e: e.reciprocal(out=rsd_t, in_=rsd_t), ['rsd'], ['rsd'])

                def conv_a_tail(h):
                    for i in range(3):
                        tt('dve', a_t[:, i, :], a_t[:, i, :], mu_t, ALU.subtract, [('a', i), 'mu'], [('a', i)])
                        tt('dve', a_t[:, i, :], a_t[:, i, :], rsd_t, ALU.mult, [('a', i), 'rsd'], [('a', i)])
                        act(cat[:, i, h * 512:(h + 1) * 512], a_t[:, i, :], AF.Silu, [('a', i), 'misc'],
                            [('cat', i, 4 * h + cc) for cc in range(4)],
                            bias=misc[:, i, 8 + l:8 + l + 1], scale=misc[:, i, 4 + l:4 + l + 1])

                b_part1(0)
                flush_pdefer(4)
                conv_a(0)
                b_part2(0)
                b_part1(1)
                flush_pdefer()
                conv_a_tail(0)
                conv_a(1)
                b_part2(1)
                conv_a_tail(1)
                s4 = wget(('in', l, 4))
                s5 = wget(('in', l, 5))
                for h in range(2):
                    for i in range(3):
                        bc_ = nextpf()
                        bx = nextpf()
                        for k in range(8):
                            mm(pf[bc_][:, :], ring[s4][:, k, i * 128:(i + 1) * 128], xT[:, k, h * 512:(h + 1) * 512], k == 0, k == 7,
                               [('w', s4)] + xT_res(h), [('pf', bc_)])
                        for k in range(8):
                            mm(pf[bx][:, :], ring[s5][:, k, i * 128:(i + 1) * 128], xT[:, k, h * 512:(h + 1) * 512], k == 0, k == 7,
                               [('w', s5)] + xT_res(h), [('pf', bx)])
                        sb_ = (h * 3 + i) % 2
                        act(sigt[:, sb_, :], pf[bx][:, :], AF.Copy, [('pf', bx)], [('sigt', sb_)])
                        tt('dve', prod[:, i, 2 + h * 512:2 + (h + 1) * 512], pf[bc_][:, :], sigt[:, sb_, :], ALU.mult,
                           [('pf', bc_), ('sigt', sb_)], [('prod', i, h)])
                cp('pool', pstate[:, l, :, :], prod[:, :, 1024:1026], [('prod', i, 1) for i in range(3)], [('pstate', l)])
                s3 = wget(('in', l, 3))
                for h in range(2):
                    for i in range(3):
                        bb = nextpf()
                        bk = nextpf()
                        for k in range(8):
                            mm(pf[bb][:, :], ring[s3][:, k, i * 128:(i + 1) * 128], xT[:, k, h * 512:(h + 1) * 512], k == 0, k == 7,
                               [('w', s3)] + xT_res(h), [('pf', bb)])
                        rr = [('prod', i, 0)] + ([('prod', i, 'halo')] if h == 0 else [('prod', i, 1)])
                        for k in range(3):
                            mm(pf[bk][:, :], dgc[:, i * 3 + k, :], prod[:, i, h * 512 + k:h * 512 + k + 512], k == 0, k == 2,
                               rr + [('dgc', i, k)], [('pf', bk)])
                        sb_ = (h * 3 + i) % 2
                        act(sigt[:, sb_, :], pf[bk][:, :], AF.Copy, [('pf', bk)], [('sigt', sb_)])
                        tt('dve', cat[:, 5 + i, h * 512:(h + 1) * 512], pf[bb][:, :], sigt[:, sb_, :], ALU.mult,
                           [('pf', bb), ('sigt', sb_)], [('cat', 5 + i, 4 * h + cc) for cc in range(4)])
                umark()
                umark('B')
                load_gbt(l, 0)

                lp1 = LNPipe(True, (0 * 4 + l, 1 * 4 + l))

                def after1(hf, c):
                    if hf == 1:
                        lp1.push(c)
                proj_out('out', l, after1)
                lp1.flush()
                k_proj(l)
                make_xT(0, 0 * 4 + l, 1 * 4 + l)
                v_proj(l)

                sq = [wget(('q', l, 0)), wget(('q', l, 1))]
                evq = [0]

                def q_group(blk, j, h):
                    b = nextpf()
                    s_ = sq[blk]
                    for k in range(8):
                        mm(pf[b][:, :], ring[s_][:, k, j * 128:(j + 1) * 128], xT[:, k, h * 512:(h + 1) * 512], k == 0, k == 7,
                           [('w', s_)] + xT_res(h), [('pf', b)])
                    if evq[0] % 2 == 0:
                        act(qT[:, blk * 4 + j, h * 512:(h + 1) * 512], pf[b][:, :], AF.Copy, [('pf', b)], [('qT', blk * 4 + j, h)])
                    else:
                        cp('dve', qT[:, blk * 4 + j, h * 512:(h + 1) * 512], pf[b][:, :], [('pf', b)], [('qT', blk * 4 + j, h)])
                    evq[0] += 1
                flush_pdefer()
                for blk in range(2):
                    for j in range(4):
                        q_group(blk, j, 0)
                make_xT(1, 0 * 4 + l, 1 * 4 + l)

                def att_scores(h, cc):
                    c = 4 * h + cc
                    par = cc % 2
                    bs_ = [nextpf(), nextpf()]
                    for hd in range(4):
                        for dc in range(2):
                            mm(pf[bs_[hd // 2]][:, (hd % 2) * 256:(hd % 2 + 1) * 256], qT[:, hd * 2 + dc, c * 128:(c + 1) * 128],
                               KT[:, hd * 2 + dc, :], dc == 0, dc == 1, [('qT', hd * 2 + dc, h), 'KT'], [('pf', bs_[hd // 2])])
                    A = [('att', par)]
                    for q2 in range(2):
                        p.op('dve', lambda e, q2=q2, par=par, bq=bs_[q2]: e.tensor_reduce(
                            out=amx[:, par, q2 * 2:q2 * 2 + 2], in_=pf[bq][:, :].rearrange("p (h n) -> p h n", h=2),
                            axis=AX.X, op=ALU.max), [('pf', bs_[q2])], A)
                    ts('dve', amx[:, par, 4:8], amx[:, par, 0:4], -1.0 / 16, None, ALU.mult, None, A, A)
                    for hd in range(4):
                        act(pe_[:, par, hd, :], pf[bs_[hd // 2]][:, (hd % 2) * 256:(hd % 2 + 1) * 256], AF.Exp,
                            [('pf', bs_[hd // 2])] + A, [('pe', par, hd)] + A,
                            bias=amx[:, par, 4 + hd:5 + hd], scale=1.0 / 16, accum=asum[:, par, hd:hd + 1])
                    p.op('dve', lambda e, par=par: e.reciprocal(out=asum[:, par, 4:8], in_=asum[:, par, 0:4]), A, A)
                    for hd in range(4):
                        ts('dve', pn[:, par, hd, :], pe_[:, par, hd, :], asum[:, par, 4 + hd:5 + hd], None, ALU.mult, None,
                           [('pe', par, hd)] + A, [('pn', par)])

                def att_tr(h, cc):
                    par = cc % 2
                    b = nextpb()
                    for hd in range(4):
                        for mc in range(2):
                            tr(pb[b][:, (hd * 2 + mc) * 128:(hd * 2 + mc + 1) * 128], pn[:, par, hd, mc * 128:(mc + 1) * 128], identb[:],
                               [('pn', par), 'identb'], [('pb', b)])
                    if cc % 2 == 0:
                        act(pT[:, h, :, :, cc * 128:(cc + 1) * 128], pb[b][:, :].rearrange("p (h m n) -> p h m n", h=4, m=2), AF.Copy,
                            [('pb', b)], [('pT', h, cc)])
                    else:
                        cp('dve', pT[:, h, :, :, cc * 128:(cc + 1) * 128], pb[b][:, :].rearrange("p (h m n) -> p h m n", h=4, m=2),
                           [('pb', b)], [('pT', h, cc)])

                def pv_group(h, g8):
                    hd, dc = g8 // 2, g8 % 2
                    b = nextpf()
                    for mc in range(2):
                        mm(pf[b][:, :], Vt[:, mc, hd * 256 + dc * 128:hd * 256 + (dc + 1) * 128], pT[:, h, hd, mc, :], mc == 0, mc == 1,
                           ['Vt'] + [('pT', h, cc) for cc in range(4)], [('pf', b)])
                    wr = [('cat', hd * 2 + dc, 4 * h + cc) for cc in range(4)]
                    if dc == 0:
                        act(cat[:, hd * 2 + dc, h * 512:(h + 1) * 512], pf[b][:, :], AF.Copy, [('pf', b)], wr)
                    else:
                        cp('dve', cat[:, hd * 2 + dc, h * 512:(h + 1) * 512], pf[b][:, :], [('pf', b)], wr)

                seqc = [(h, cc) for h in range(2) for cc in range(4)]
                qfill = [(blk, j) for blk in range(2) for j in range(4)]
                qsched = {0: qfill[0:2], 1: qfill[2:4], 2: qfill[4:6], 3: qfill[6:8]}
                att_scores(*seqc[0])
                for idx in range(8):
                    h, cc = seqc[idx]
                    if h == 0:
                        for (blk, j) in qsched[cc]:
                            q_group(blk, j, 1)
                    if idx + 1 < 8:
                        att_scores(*seqc[idx + 1])
                    if h == 1:
                        pv_group(0, 2 * cc)
                        pv_group(0, 2 * cc + 1)
                    att_tr(h, cc)
                for g8 in range(8):
                    pv_group(1, g8)
                umark()
                load_gbt(l, 1)

                lp2 = LNPipe(True, (2 * 4 + l, 3 * 4 + l))

                def after2(hf, c):
                    if hf == 1:
                        lp2.push(c)
                proj_out('o', l, after2)
                lp2.flush()
                make_xT(0, 2 * 4 + l, 3 * 4 + l)


                def store_chunk(c, ti=ti, row0=row0):
                    dma('sp', out[row0 + c * 128:row0 + (c + 1) * 128, :], x_tm[:, c, :],
                        [('x', c)], [('o', ti, c)], f'xout{c}')
                    if ti + 1 < NT:
                        r1 = (ti + 1) * T
                        dma('sp', x_tm[:, c, :], x[r1 + c * 128:r1 + (c + 1) * 128, :], [], [('x', c)], f'xin{c}')
                lp3 = LNPipe(not lastl, None if lastl else (4 * 4 + l, 5 * 4 + l), store_chunk if lastl else None)
                ri = 0
                for g in range(2):
                    if g == 1:
                        load_gbt(l, 2)
                    for blk in range(4):
                        s = wget(('f1', l, g, blk))
                        if g == 0 and blk in (1, 2):
                            flush_pdefer(4)
                        if g == 0 and blk == 0:
                            order = [(j, 0) for j in range(4)] + ['x1'] + [(j, 1) for j in range(4)]
                        else:
                            order = [(j, h) for j in range(4) for h in range(2)]
                        for it in order:
                            if it == 'x1':
                                make_xT(1, 2 * 4 + l, 3 * 4 + l)
                                continue
                            j, h = it
                            if True:
                                b = nextpf()
                                for k in range(8):
                                    mm(pf[b][:, :], ring[s][:, k, j * 128:(j + 1) * 128], xT[:, k, h * 512:(h + 1) * 512], k == 0, k == 7,
                                       [('w', s)] + xT_res(h), [('pf', b)])
                                rb = ri % 2
                                ri += 1
                                act(rt[:, rb, :], pf[b][:, :], AF.Relu, [('pf', b)], [('rt', rb)])
                                tt('dve', hT[:, blk * 4 + j, h * 512:(h + 1) * 512], rt[:, rb, :], pf[b][:, :], ALU.mult,
                                   [('rt', rb), ('pf', b)], [('hT', blk * 4 + j, h)])
                    for cb in range(2):
                        sa = wget(('f2', l, g, cb, 0))
                        sb2 = wget(('f2', l, g, cb, 1))
                        for c in range(NCH):
                            b = nextpf()
                            for kk in range(2):
                                s = sa if kk == 0 else sb2
                                for k in range(8):
                                    mm(pf[b][:, :], hT[:, kk * 8 + k, c * 128:(c + 1) * 128], ring[s][:, k, 0:512],
                                       kk == 0 and k == 0, kk == 1 and k == 7,
                                       [('w', s), ('hT', kk * 8 + k, c // 4)], [('pf', b)])
                            xs = x_tm[:, c, cb * 512:(cb + 1) * 512]
                            if g == 0:
                                stt(xs, xs, ALPHA, pf[b][:, :], ALU.mult, ALU.add, [('x', c), ('pf', b)], [('x', c)])
                            else:
                                tt('dve', xs, xs, pf[b][:, :], ALU.add, [('x', c), ('pf', b)], [('x', c)])
                            if g == 1 and cb == 1:
                                lp3.push(c)
                lp3.flush()
                if not lastl:
                    make_xT(0, 4 * 4 + l, 5 * 4 + l)
                    pend['x1'] = (4 * 4 + l, 5 * 4 + l)
                umark()
        p.op('sp', None, [('o', ti, c) for ti in range(NT) for c in range(NCH)], [])
        assert wstate['use'] == len(specs)
        p.emit(nc, stack)
    nc._dbg = p.dbg
    return nc


WNAMES = ['w_in', 'conv_a_w', 'conv_a_b', 'ln_a_g', 'ln_a_b', 'ln_v_g', 'ln_v_b', 'w_s', 'b_s', 'conv_c_w', 'w_out',
          'ln1_g', 'ln1_b', 'w_q', 'w_kv', 'w_o', 'ln2_g', 'ln2_b', 'w_ff1', 'w_ff2', 'ln3_g', 'ln3_b']


def prep_weights(inputs):
    wd = {}
    for n in WNAMES:
        a = np.ascontiguousarray(np.asarray(inputs[n], dtype=np.float32))
        if n == 'conv_a_w':
            a = a.reshape(DEPTH * 31, 384)
        elif n == 'conv_c_w':
            a = a.reshape(DEPTH * 3, 384)
        wd[n] = a
    return wd


def kernel(**inputs):
    ncores = 8
    x = np.asarray(inputs['x'], dtype=np.float32)
    mem = np.asarray(inputs['mem'], dtype=np.float32)
    B = x.shape[0]
    nseq = B // ncores
    wd = prep_weights(inputs)
    nc = build(nseq, DEPTH)
    in_maps = []
    for c in range(ncores):
        m = dict(wd)
        m['x'] = np.ascontiguousarray(x[c * nseq:(c + 1) * nseq].reshape(nseq * S, D))
        m['mem'] = np.ascontiguousarray(mem[c * nseq:(c + 1) * nseq].reshape(nseq * MEM, D))
        in_maps.append(m)
    res = run_bass_kernel_spmd(nc, in_maps, core_ids=list(range(ncores)))
    outs = [np.asarray(r['out']).reshape(nseq, S, D) for r in res.results]
    return np.concatenate(outs, axis=0).astype(np.float32)
```

```python
import numpy as np
from contextlib import ExitStack
import concourse.bass as bass
import concourse.mybir as mybir
from concourse.bass_utils import run_bass_kernel_spmd

F32 = mybir.dt.float32
BF16 = mybir.dt.bfloat16
AF = mybir.ActivationFunctionType
ALU = mybir.AluOpType
AX = mybir.AxisListType

D = 1024
S = 2048
T = 1024
NCH = T // 128
MEM = 256
DEPTH = 4
ALPHA = (2.0 * DEPTH) ** 0.25
EPS = 1e-5
NSLOT = 4
NPF = 6
ENGS = ['pe', 'act', 'dve', 'pool', 'sp']
CENGS = ['pe', 'act', 'dve', 'pool']
CH = 16000
U_KINDS = {'glu', 'prod', 'dg', 'dgc', 'vz', 'vzc', 'u', 'vg', 'a', 'asq', 'sigt', 'tmpb', 'mu', 'var', 'rsd',
           'qT', 'pT', 'pe', 'pn', 'KT', 'Vt', 'hT', 'rt', 'memf', 'memb', 'wsf', 'wsb', 'R1', 'R2', 'R3'}
ZONES = {'A': U_KINDS, 'B': {'nbh', 'a', 'asq'}}


class Prog:
    def __init__(self):
        self.q = {e: [] for e in ENGS}
        self.cnt = {e: 0 for e in ENGS}
        self.lastw = {}
        self.readers = {}
        self.seen = {e: {} for e in ENGS}
        self.dma_cnt = {}
        self.u_users = {z: {} for z in ZONES}
        self.u_deps = {z: {} for z in ZONES}
        self.dbg = []
        self.tag = ''

    def umark(self, z='A'):
        for k, v in self.u_users[z].items():
            if self.u_deps[z].get(k, 0) < v:
                self.u_deps[z][k] = v
        self.u_users[z] = {}

    def op(self, eng, fn, reads=(), writes=(), stream=None):
        kinds = set((r[0] if isinstance(r, tuple) else r) for r in list(reads) + list(writes))
        zt = [z for z in ZONES if kinds & ZONES[z]]
        deps = {}
        for z in zt:
            for k, v in self.u_deps[z].items():
                if deps.get(k, 0) < v:
                    deps[k] = v
        for r in reads:
            t = self.lastw.get(r)
            if t is not None and deps.get(t[0], 0) < t[1]:
                deps[t[0]] = t[1]
        for r in writes:
            t = self.lastw.get(r)
            if t is not None and deps.get(t[0], 0) < t[1]:
                deps[t[0]] = t[1]
            for k, v in self.readers.get(r, {}).items():
                if deps.get(k, 0) < v:
                    deps[k] = v
        if fn is None:
            me = None
        elif stream is None:
            self.cnt[eng] += 1
            me = (eng, self.cnt[eng])
        else:
            key = ('dma', stream)
            self.dma_cnt[key] = self.dma_cnt.get(key, 0) + 1
            me = (key, self.dma_cnt[key])
        waits = []
        seen = self.seen[eng]
        for k, v in deps.items():
            if k == 'pe' and eng == 'pe':
                continue
            if seen.get(k, 0) >= v:
                continue
            seen[k] = v
            waits.append((k, v))
        self.q[eng].append((waits, fn, me))
        self.dbg.append((eng, self.tag, me, tuple(waits)))
        if me is not None:
            for z in zt:
                if self.u_users[z].get(me[0], 0) < me[1]:
                    self.u_users[z][me[0]] = me[1]
        if me is not None:
            for r in reads:
                d = self.readers.setdefault(r, {})
                if d.get(me[0], 0) < me[1]:
                    d[me[0]] = me[1]
            for r in writes:
                self.lastw[r] = me
                self.readers[r] = {}
        return me

    def barrier(self):
        snap = {e: self.cnt[e] for e in CENGS}
        for e in ENGS:
            waits = []
            for k, v in snap.items():
                if v == 0 or self.seen[e].get(k, 0) >= v:
                    continue
                self.seen[e][k] = v
                waits.append((k, v))
            if waits:
                self.q[e].append((waits, None, None))

    def emit(self, nc, stack):
        needed = set()
        for e in ENGS:
            for waits, fn, me in self.q[e]:
                for w in waits:
                    needed.add(w)
        sig = {}
        nsig = {}
        for e in ENGS:
            n = 0
            for waits, fn, me in self.q[e]:
                if me is not None and me[0] == e and me in needed:
                    n += 1
                    sig[me] = n
            nsig[e] = n
        sems = {}
        for e in ENGS:
            sems[e] = [stack.enter_context(nc.semaphore(f"s_{e}_{i}")) for i in range((nsig[e] + CH - 1) // CH)]
        dsems = {k: stack.enter_context(nc.semaphore("d_" + str(k[1]))) for k in self.dma_cnt}

        def semval(k, v):
            if isinstance(k, tuple):
                return dsems[k], 16 * v
            idx = sig[(k, v)]
            return sems[k][(idx - 1) // CH], (idx - 1) % CH + 1

        block = stack.enter_context(nc.Block())
        q = self.q

        def run(engname):
            def body(eng):
                for waits, fn, me in q[engname]:
                    for k, v in waits:
                        s, val = semval(k, v)
                        eng.wait_ge(s, val)
                    if fn is None:
                        continue
                    ins = fn(eng)
                    if isinstance(me[0], tuple):
                        ins.then_inc(dsems[me[0]], 16)
                    elif me in sig:
                        s, val = semval(*me)
                        ins.then_inc(s, 1)
            return body
        block.tensor(run('pe'))
        block.scalar(run('act'))
        block.vector(run('dve'))
        block.gpsimd(run('pool'))
        block.sync(run('sp'))


def weight_specs(nl):
    sp = []
    for l in range(nl):
        for b in (0, 1, 2, 4, 5, 3):
            sp.append(('in', l, b))
        for hf in range(2):
            sp.append(('out', l, hf))
        for b in range(2):
            sp.append(('k', l, b))
        for b in range(2):
            sp.append(('v', l, b))
        for b in range(2):
            sp.append(('q', l, b))
        for hf in range(2):
            sp.append(('o', l, hf))
        for g in range(2):
            for b in range(4):
                sp.append(('f1', l, g, b))
            for cb in range(2):
                for kk in range(2):
                    sp.append(('f2', l, g, cb, kk))
    return sp


IN_COLS = {0: (0, 384), 1: (384, 384), 2: (768, 512), 3: (1280, 384), 4: (1664, 384), 5: (2048, 384)}


def build(nseq, nl, S=S):
    nc = bass.Bass("TRN2", target_bir_lowering=False)
    NT = nseq * S // T
    TPS = S // T

    def din(name, shape):
        return nc.dram_tensor(name, shape, F32, kind="ExternalInput").ap()
    x = din("x", [nseq * S, D])
    mem = din("mem", [nseq * MEM, D])
    w_in = din("w_in", [DEPTH, D, 2432])
    conv_a_w = din("conv_a_w", [DEPTH * 31, 384])
    conv_a_b = din("conv_a_b", [DEPTH, 384])
    ln_a_g = din("ln_a_g", [DEPTH, 384])
    ln_a_b = din("ln_a_b", [DEPTH, 384])
    ln_v_g = din("ln_v_g", [DEPTH, 256])
    ln_v_b = din("ln_v_b", [DEPTH, 256])
    w_s = din("w_s", [DEPTH, 4, 128, 128])
    b_s = din("b_s", [DEPTH, 4, 128])
    conv_c_w = din("conv_c_w", [DEPTH * 3, 384])
    w_out = din("w_out", [DEPTH, D, D])
    lng = [din("ln1_g", [DEPTH, D]), din("ln1_b", [DEPTH, D]), din("ln2_g", [DEPTH, D]),
           din("ln2_b", [DEPTH, D]), din("ln3_g", [DEPTH, D]), din("ln3_b", [DEPTH, D])]
    w_q = din("w_q", [DEPTH, D, D])
    w_kv = din("w_kv", [DEPTH, D, 2 * D])
    w_o = din("w_o", [DEPTH, D, D])
    w_ff1 = din("w_ff1", [DEPTH, D, 4 * D])
    w_ff2 = din("w_ff2", [DEPTH, 4 * D, D])
    out = nc.dram_tensor("out", [nseq * S, D], F32, kind="ExternalOutput").ap()

    stack = ExitStack()
    with stack:
        def sb(name, shape, dt):
            return stack.enter_context(nc.sbuf_tensor(name, shape, dt))
        x_tm = sb("x_tm", [128, NCH, D], F32)
        xT = sb("xT", [128, 8, T], BF16)
        nb = sb("nb", [128, 4, D], BF16)
        cat = sb("cat", [128, 8, T], BF16)
        ring = [sb(f"ring{i}", [128, 8, 512], BF16) for i in range(NSLOT)]
        gbt = sb("gbt", [128, 2, D], F32)
        memT = sb("memT", [128, 8, MEM], BF16)
        wsT = sb("wsT", [128, 16, 128], BF16)
        gbT = sb("gbT", [128, 8, 24], F32)
        caw = sb("caw", [128, 3, 124], F32)
        misc = sb("misc", [128, 3, 24], F32)
        lvgb = sb("lvgb", [128, 2, 256], F32)
        bst = sb("bst", [128, 2, 128], F32)
        identb = sb("identb", [128, 128], BF16)
        identf = sb("identf", [128, 128], F32)
        onesf = sb("onesf", [128, 128], F32)
        maskc = sb("maskc", [128, 128], F32)
        epst = sb("epst", [128, 1], F32)
        gstate = sb("gstate", [128, DEPTH, 3, 30], BF16)
        pstate = sb("pstate", [128, DEPTH, 3, 2], BF16)
        lnst = sb("lnst", [128, NCH, 12], F32)
        lnmv = sb("lnmv", [128, NCH, 2], F32)
        lnsd = sb("lnsd", [128, NCH, 3], F32)
        vst = sb("vst", [128, 4, 6], F32)
        vmv = sb("vmv", [128, 4, 4], F32)
        amx = sb("amx", [128, 2, 12], F32)
        asum = sb("asum", [128, 2, 8], F32)
        scr = sb("scr", [128, 2], F32)
        UW = 37248
        U = sb("U", [128, UW], BF16)
        pf = [stack.enter_context(nc.psum_tensor(f"pf{i}", [128, 512], F32)) for i in range(NPF)]
        pb = [stack.enter_context(nc.psum_tensor(f"pb{i}", [128, 1024], BF16)) for i in range(2)]

        def ub(off, n):
            return U[:, off:off + n]

        def uf(off, n):
            return U[:, off:off + 2 * n].bitcast(F32)
        o = 0
        glu = ub(o, 3 * 1056).rearrange("p (i n) -> p i n", i=3); o += 3 * 1056
        prod = ub(o, 3 * 1028).rearrange("p (i n) -> p i n", i=3); o += 3 * 1028
        dg = ub(o, 93 * 128).rearrange("p (k n) -> p k n", k=93); o += 93 * 128
        dgc = ub(o, 9 * 128).rearrange("p (k n) -> p k n", k=9); o += 9 * 128
        vz = ub(o, 2048); o += 2048
        u_t = uf(o, 1024).rearrange("p (j n) -> p j n", j=2); o += 2048
        vg = uf(o, 1024).rearrange("p (c n) -> p c n", c=4); o += 2048
        a_t = uf(o, 1536).rearrange("p (i n) -> p i n", i=3); o += 3072
        asq = uf(o, 1536).rearrange("p (i n) -> p i n", i=3); o += 3072
        sigt = uf(o, 1024).rearrange("p (b n) -> p b n", b=2); o += 2048
        tmpb = uf(o, 256); o += 512
        mu_t = uf(o, 512); o += 1024
        var_t = uf(o, 512); o += 1024
        rsd_t = uf(o, 512); o += 1024
        assert o <= UW, o
        o = 0
        qT = ub(o, 8 * T).rearrange("p (f n) -> p f n", f=8); o += 8 * T
        pT = ub(o, 8192).rearrange("p (g h m n) -> p g h m n", g=2, h=4, m=2); o += 8192
        pe_ = uf(o, 2048).rearrange("p (b h n) -> p b h n", b=2, h=4); o += 4096
        pn = ub(o, 2048).rearrange("p (b h n) -> p b h n", b=2, h=4); o += 2048
        KT = ub(o, 2048).rearrange("p (f n) -> p f n", f=8); o += 2048
        Vt = ub(o, 2048).rearrange("p (m n) -> p m n", m=2); o += 2048
        assert o <= 26624
        nbh = ub(26624, 4096).rearrange("p (c n) -> p c n", c=4)
        o = 0
        hT = ub(o, 16 * T).rearrange("p (f n) -> p f n", f=16); o += 16 * T
        rt = uf(o, 1024).rearrange("p (b n) -> p b n", b=2); o += 2048
        assert o <= UW
        memf = uf(0, 2048).rearrange("p (m n) -> p m n", m=2)
        memb = ub(4096, 2048).rearrange("p (m n) -> p m n", m=2)
        wsf = uf(0, 2048).rearrange("p (k n) -> p k n", k=16)
        wsb = ub(4096, 2048).rearrange("p (k n) -> p k n", k=16)
        R1 = uf(8192, 1024)
        R2 = uf(8192 + 2048, 384)
        R3 = uf(8192 + 2048 + 768, 384)

        p = Prog()
        pfi = [0]

        def nextpf():
            b = pfi[0]
            pfi[0] = (b + 1) % NPF
            return b
        pbi = [0]

        def nextpb():
            b = pbi[0]
            pbi[0] = (b + 1) % 2
            return b

        def mm(o_, lhsT, rhs, start, stop, reads, writes):
            p.op('pe', lambda e: e.matmul(o_, lhsT=lhsT, rhs=rhs, start=start, stop=stop), reads, writes)

        def tr(o_, in_, ident, reads, writes):
            p.op('pe', lambda e: e.transpose(out=o_, in_=in_, identity=ident), reads, writes)

        def act(o_, in_, func, reads, writes, bias=None, scale=None, accum=None):
            kw = {}
            if bias is not None:
                kw['bias'] = bias
            if scale is not None:
                kw['scale'] = scale
            if accum is not None:
                kw['accum_out'] = accum
            p.op('act', lambda e: e.activation(out=o_, in_=in_, func=func, **kw), reads, writes)

        def tt(eng, o_, in0, in1, op, reads, writes):
            p.op(eng, lambda e: e.tensor_tensor(out=o_, in0=in0, in1=in1, op=op), reads, writes)

        def ts(eng, o_, in0, s1, s2, op0, op1, reads, writes):
            if op1 is None:
                p.op(eng, lambda e: e.tensor_scalar(out=o_, in0=in0, scalar1=s1, scalar2=None, op0=op0), reads, writes)
            else:
                p.op(eng, lambda e: e.tensor_scalar(out=o_, in0=in0, scalar1=s1, scalar2=s2, op0=op0, op1=op1), reads, writes)

        def stt(o_, in0, sc, in1, op0, op1, reads, writes):
            p.op('dve', lambda e: e.scalar_tensor_tensor(out=o_, in0=in0, scalar=sc, in1=in1, op0=op0, op1=op1), reads, writes)

        def cp(eng, o_, in_, reads, writes):
            p.op(eng, lambda e: e.tensor_copy(out=o_, in_=in_), reads, writes)

        def dma(eng, o_, in_, reads, writes, stream):
            p.op(eng, lambda e: e.dma_start(out=o_, in_=in_), reads, writes, stream=stream)

        specs = []
        for ti in range(NT):
            specs += weight_specs(nl)
        wstate = {'use': 0, 'iss': 0}

        def wsrc(key):
            kind, l = key[0], key[1]
            if kind == 'in':
                c0, n = IN_COLS[key[2]]
                return w_in[l, :, c0:c0 + n], n
            if kind == 'out':
                return w_out[l, :, key[2] * 512:(key[2] + 1) * 512], 512
            if kind == 'k':
                return w_kv[l, :, key[2] * 512:(key[2] + 1) * 512], 512
            if kind == 'v':
                return w_kv[l, :, 1024 + key[2] * 512:1024 + (key[2] + 1) * 512], 512
            if kind == 'q':
                return w_q[l, :, key[2] * 512:(key[2] + 1) * 512], 512
            if kind == 'o':
                return w_o[l, :, key[2] * 512:(key[2] + 1) * 512], 512
            if kind == 'f1':
                c0 = key[2] * 2048 + key[3] * 512
                return w_ff1[l, :, c0:c0 + 512], 512
            if kind == 'f2':
                r0 = key[2] * 2048 + key[4] * 1024
                return w_ff2[l, r0:r0 + 1024, key[3] * 512:(key[3] + 1) * 512], 512
            raise ValueError(key)

        def wissue(j):
            src, n = wsrc(specs[j])
            slot = j % NSLOT
            dma('pool', ring[slot][:, :, 0:n], src.rearrange("(k p) n -> p k n", p=128), [], [('w', slot)], f"w{slot}")

        def wget(key):
            i = wstate['use']
            assert specs[i] == key, (specs[i], key)
            lim = min(len(specs), i + NSLOT - 1)
            while wstate['iss'] < lim:
                wissue(wstate['iss'])
                wstate['iss'] += 1
            wstate['use'] += 1
            return i % NSLOT

        p.op('pool', lambda e: e.memset(identb[:], 1.0), [], ['identb'])
        p.op('pool', lambda e: e.affine_select(out=identb[:], in_=identb[:], pattern=[[-1, 128]], compare_op=ALU.is_equal,
                                                fill=0.0, base=0, channel_multiplier=1), ['identb'], ['identb'])
        p.op('pool', lambda e: e.memset(identf[:], 1.0), [], ['identf'])
        p.op('pool', lambda e: e.affine_select(out=identf[:], in_=identf[:], pattern=[[-1, 128]], compare_op=ALU.is_equal,
                                                fill=0.0, base=0, channel_multiplier=1), ['identf'], ['identf'])
        p.op('pool', lambda e: e.memset(onesf[:], 1.0), [], ['onesf'])
        p.op('pool', lambda e: e.memset(epst[:], EPS), [], ['epst'])
        p.op('pool', lambda e: e.memset(maskc[:], 1.0), [], ['maskc'])
        p.op('pool', lambda e: e.affine_select(out=maskc[:], in_=maskc[:], pattern=[[1, 128]], compare_op=ALU.is_ge,
                                                fill=0.0, base=0, channel_multiplier=-1), ['maskc'], ['maskc'])
        for i6 in range(6):
            dma('sp', R1[i6 * 4:(i6 + 1) * 4, :], lng[i6], [], [('R1', i6)], f'parR1_{i6}')
        dma('sp', R2[0:124, :], conv_a_w, [], ['R2'], 'parR2')
        dma('sp', R3[0:4, :], conv_a_b, [], [('R3', 0)], 'parR30')
        dma('sp', R3[4:8, :], ln_a_g, [], [('R3', 1)], 'parR31')
        dma('sp', R3[8:12, :], ln_a_b, [], [('R3', 2)], 'parR32')
        dma('sp', R3[12:24, :], conv_c_w, [], [('R3', 3)], 'parR33')
        dma('sp', wsf, w_s.rearrange("l h t s -> t (l h) s"), [], ['wsf'], 'parws')
        for f in range(8):
            b = nextpf()
            tr(pf[b][:, 0:24], R1[0:24, f * 128:(f + 1) * 128], identf[0:24, 0:24],
               [('R1', i6) for i6 in range(6)] + ['identf'], [('pf', b)])
            cp('dve', gbT[:, f, :], pf[b][:, 0:24], [('pf', b)], ['gbT'])
        for i in range(3):
            b = nextpf()
            tr(pf[b][:, 0:124], R2[0:124, i * 128:(i + 1) * 128], identf[0:124, 0:124], ['R2', 'identf'], [('pf', b)])
            cp('dve', caw[:, i, :], pf[b][:, 0:124], [('pf', b)], ['caw'])
            b = nextpf()
            tr(pf[b][:, 0:24], R3[0:24, i * 128:(i + 1) * 128], identf[0:24, 0:24],
               [('R3', k) for k in range(4)] + ['identf'], [('pf', b)])
            cp('dve', misc[:, i, :], pf[b][:, 0:24], [('pf', b)], ['misc'])
        cp('dve', wsb, wsf, ['wsf'], ['wsb'])
        for q8 in range(2):
            b = nextpb()
            for k in range(8):
                tr(pb[b][:, k * 128:(k + 1) * 128], wsb[:, q8 * 8 + k, :], identb[:], ['wsb', 'identb'], [('pb', b)])
            for k in range(8):
                tt('dve', wsT[:, q8 * 8 + k, :], pb[b][:, k * 128:(k + 1) * 128], maskc[:], ALU.mult,
                   [('pb', b), 'maskc'], ['wsT'])

        def umark(z='A'):
            p.umark(z)

        def load_gbt(l, n):
            dma('sp', gbt[:, 0, :], lng[2 * n][l:l + 1, :].partition_broadcast(128), [], [('gbt', 0)], 'gbt0')
            dma('sp', gbt[:, 1, :], lng[2 * n + 1][l:l + 1, :].partition_broadcast(128), [], [('gbt', 1)], 'gbt1')

        def nbs(c):
            return nb[:, c, :] if c < 4 else nbh[:, c - 4, :]

        def nbr(c):
            return ('nb', c) if c < 4 else ('nbh', c)

        def make_xT(h, gcol, bcol):
            for fp in range(4):
                b = nextpb()
                for f in (2 * fp, 2 * fp + 1):
                    off = (f % 2) * 512
                    for cc in range(4):
                        c = 4 * h + cc
                        tr(pb[b][:, off + cc * 128:off + (cc + 1) * 128], nbs(c)[:, f * 128:(f + 1) * 128], identb[:],
                           [nbr(c), 'identb'], [('pb', b)])
                for f in (2 * fp, 2 * fp + 1):
                    off = (f % 2) * 512
                    wr = [('xT', 4 * h + cc) for cc in range(4)]
                    if gcol is None:
                        act(xT[:, f, h * 512:(h + 1) * 512], pb[b][:, off:off + 512], AF.Copy, [('pb', b)], wr)
                    else:
                        act(xT[:, f, h * 512:(h + 1) * 512], pb[b][:, off:off + 512], AF.Identity,
                            [('pb', b), 'gbT'], wr,
                            bias=gbT[:, f, bcol:bcol + 1], scale=gbT[:, f, gcol:gcol + 1])

        def ln_s1(c):
            R = [('x', c)]
            L = [('lnst', c)]
            p.op('dve', lambda e: e.bn_stats(out=lnst[:, c, 0:6], in_=x_tm[:, c, 0:512]), R, L)
            p.op('dve', lambda e: e.bn_stats(out=lnst[:, c, 6:12], in_=x_tm[:, c, 512:1024]), R, L)
            p.op('dve', lambda e: e.bn_aggr(out=lnmv[:, c, :], in_=lnst[:, c, :]), L, L)
            act(lnsd[:, c, 0:1], lnmv[:, c, 1:2], AF.Sqrt, L + ['epst'], L, bias=epst[:, 0:1])

        def ln_s2(c, want_nb):
            R = [('x', c)]
            L = [('lnst', c)]
            p.op('dve', lambda e: e.reciprocal(out=lnsd[:, c, 1:2], in_=lnsd[:, c, 0:1]), L, L)
            ts('dve', lnsd[:, c, 2:3], lnmv[:, c, 0:1], -1.0, lnsd[:, c, 1:2], ALU.mult, ALU.mult, L, L)
            if want_nb:
                act(nbs(c), x_tm[:, c, :], AF.Identity, R + L, [nbr(c)],
                    bias=lnsd[:, c, 2:3], scale=lnsd[:, c, 1:2])

        pdefer = []

        def flush_pdefer(n=None):
            k = len(pdefer) if n is None else min(n, len(pdefer))
            for _ in range(k):
                pdefer.pop(0)()

        def ln_s3(c, fast=False):
            xs = x_tm[:, c, :]
            R = [('x', c)]
            L = [('lnst', c)]
            if fast:
                act(xs, xs, AF.Identity, R + L, R, bias=lnsd[:, c, 2:3], scale=lnsd[:, c, 1:2])
                tt('dve', xs, xs, gbt[:, 0, :], ALU.mult, R + [('gbt', 0)], R)
                tt('pool', xs, xs, gbt[:, 1, :], ALU.add, R + [('gbt', 1)], R)
            else:
                def later():
                    ts('pool', xs, xs, lnsd[:, c, 1:2], lnsd[:, c, 2:3], ALU.mult, ALU.add, R + L, R)
                    tt('pool', xs, xs, gbt[:, 0, :], ALU.mult, R + [('gbt', 0)], R)
                    tt('pool', xs, xs, gbt[:, 1, :], ALU.add, R + [('gbt', 1)], R)
                pdefer.append(later)

        class LNPipe:
            def __init__(self, want_nb, xt_args, done=None):
                self.want_nb = want_nb
                self.xt_args = xt_args
                self.done = done

            def _s3(self, c):
                ln_s3(c, fast=(self.done is not None))
                if self.done is not None:
                    self.done(c)

            def push(self, c):
                ln_s1(c)
                if c >= 1:
                    ln_s2(c - 1, self.want_nb)
                if c >= 2:
                    self._s3(c - 2)

            def flush(self):
                ln_s2(NCH - 1, self.want_nb)
                self._s3(NCH - 2)
                self._s3(NCH - 1)

        def xT_res(h):
            return [('xT', 4 * h + cc) for cc in range(4)]

        def cat_res(c):
            return [('cat', f, c) for f in range(8)]

        def proj_out(kind, l, after):
            for hf in range(2):
                s = wget((kind, l, hf))
                for c in range(NCH):
                    b = nextpf()
                    for k in range(8):
                        mm(pf[b][:, :], cat[:, k, c * 128:(c + 1) * 128], ring[s][:, k, 0:512], k == 0, k == 7,
                           [('w', s)] + cat_res(c), [('pf', b)])
                    xs = x_tm[:, c, hf * 512:(hf + 1) * 512]
                    stt(xs, xs, ALPHA, pf[b][:, :], ALU.mult, ALU.add, [('x', c), ('pf', b)], [('x', c)])
                    after(hf, c)

        def k_proj(l):
            for blk in range(2):
                s = wget(('k', l, blk))
                for j in range(4):
                    b = nextpf()
                    for k in range(8):
                        mm(pf[b][:, 0:256], ring[s][:, k, j * 128:(j + 1) * 128], memT[:, k, :], k == 0, k == 7,
                           [('w', s), 'memT'], [('pf', b)])
                    if j % 2 == 0:
                        act(KT[:, blk * 4 + j, :], pf[b][:, 0:256], AF.Copy, [('pf', b)], ['KT'])
                    else:
                        cp('dve', KT[:, blk * 4 + j, :], pf[b][:, 0:256], [('pf', b)], ['KT'])

        def v_proj(l):
            for blk in range(2):
                s = wget(('v', l, blk))
                for mc in range(2):
                    b = nextpf()
                    for k in range(8):
                        mm(pf[b][:, :], memT[:, k, mc * 128:(mc + 1) * 128], ring[s][:, k, 0:512], k == 0, k == 7,
                           [('w', s), 'memT'], [('pf', b)])
                    if mc == 0:
                        cp('dve', Vt[:, mc, blk * 512:(blk + 1) * 512], pf[b][:, :], [('pf', b)], ['Vt'])
                    else:
                        act(Vt[:, mc, blk * 512:(blk + 1) * 512], pf[b][:, :], AF.Copy, [('pf', b)], ['Vt'])

        umark()
        for ti in range(NT):
            seq = ti // TPS
            first = (ti % TPS == 0)
            row0 = ti * T
            if first:
                dma('sp', memf, mem[seq * MEM:(seq + 1) * MEM, :].rearrange("(m p) d -> p m d", p=128), [], ['memf'], 'mem')
            if ti == 0:
                for c in range(NCH):
                    dma('sp', x_tm[:, c, :], x[row0 + c * 128:row0 + (c + 1) * 128, :], [], [('x', c)], f'xin{c}')
            if first:
                cp('dve', memb, memf, ['memf'], ['memb'])
                for q4 in range(2):
                    b = nextpb()
                    for fq in range(4):
                        for mc in range(2):
                            f = q4 * 4 + fq
                            tr(pb[b][:, (fq * 2 + mc) * 128:(fq * 2 + mc + 1) * 128], memb[:, mc, f * 128:(f + 1) * 128],
                               identb[:], ['memb', 'identb'], [('pb', b)])
                    act(memT[:, q4 * 4:(q4 + 1) * 4, :].rearrange("p f n -> p (f n)"), pb[b][:, :], AF.Copy,
                        [('pb', b)], ['memT'])
                umark()
            for h in range(2):
                for cc in range(4):
                    c = 4 * h + cc
                    if cc % 2 == 0:
                        cp('dve', nbs(c), x_tm[:, c, :], [('x', c)], [nbr(c)])
                    else:
                        act(nbs(c), x_tm[:, c, :], AF.Copy, [('x', c)], [nbr(c)])
            make_xT(0, None, None)
            pend = {'x1': (None, None)}

            for l in range(nl):
                lastl = (l == nl - 1)
                dma('sp', lvgb[:, 0, :], ln_v_g[l:l + 1, :].partition_broadcast(128), [], ['lvgb'], 'lv0')
                dma('sp', lvgb[:, 1, :], ln_v_b[l:l + 1, :].partition_broadcast(128), [], ['lvgb'], 'lv1')
                for h4 in range(4):
                    hp, j = h4 % 2, h4 // 2
                    dma('sp', bst[hp * 64:(hp + 1) * 64, j, :], b_s[l, h4:h4 + 1, :].partition_broadcast(64), [], ['bst'], f'bs{h4}')
                s0 = wget(('in', l, 0))
                s1 = wget(('in', l, 1))
                p.op('pool', lambda e: e.memset(vz, 0.0), [], ['vz'])
                if first:
                    p.op('pool', lambda e: e.memset(glu[:, :, 0:30], 0.0), [], [('glu', i, 'halo') for i in range(3)])
                    p.op('pool', lambda e: e.memset(prod[:, :, 0:2], 0.0), [], [('prod', i, 'halo') for i in range(3)])
                else:
                    cp('pool', glu[:, :, 0:30], gstate[:, l, :, :], [('gstate', l)], [('glu', i, 'halo') for i in range(3)])
                    cp('pool', prod[:, :, 0:2], pstate[:, l, :, :], [('pstate', l)], [('prod', i, 'halo') for i in range(3)])
                for i in range(3):
                    for k in range(31):
                        ts('pool', dg[:, i * 31 + k, :], identb[:], caw[:, i, l * 31 + k:l * 31 + k + 1], 0.0, ALU.mult, ALU.add,
                           ['identb', 'caw'], [('dg', i, k)])
                    for k in range(3):
                        ts('pool', dgc[:, i * 3 + k, :], identb[:], misc[:, i, 12 + l * 3 + k:12 + l * 3 + k + 1], 0.0, ALU.mult, ALU.add,
                           ['identb', 'misc'], [('dgc', i, k)])
                p.tag = f'glu t{ti} l{l}'
                for h in range(2):
                    for i in range(3):
                        bv = nextpf()
                        bg = nextpf()
                        for k in range(8):
                            mm(pf[bv][:, :], ring[s0][:, k, i * 128:(i + 1) * 128], xT[:, k, h * 512:(h + 1) * 512], k == 0, k == 7,
                               [('w', s0)] + xT_res(h), [('pf', bv)])
                        for k in range(8):
                            mm(pf[bg][:, :], ring[s1][:, k, i * 128:(i + 1) * 128], xT[:, k, h * 512:(h + 1) * 512], k == 0, k == 7,
                               [('w', s1)] + xT_res(h), [('pf', bg)])
                        sb_ = (h * 3 + i) % 2
                        act(sigt[:, sb_, :], pf[bg][:, :], AF.Sigmoid, [('pf', bg)], [('sigt', sb_)])
                        tt('dve', glu[:, i, 30 + h * 512:30 + (h + 1) * 512], pf[bv][:, :], sigt[:, sb_, :], ALU.mult,
                           [('pf', bv), ('sigt', sb_)], [('glu', i, h)])
                    if h == 0 and pend['x1'] is not None:
                        make_xT(1, *pend['x1'])
                        pend['x1'] = None
                        umark('B')
                cp('pool', gstate[:, l, :, :], glu[:, :, 1024:1054], [('glu', i, 1) for i in range(3)], [('gstate', l)])
                p.tag = f'rest t{ti} l{l}'
                s2 = wget(('in', l, 2))
                vzv = vz.rearrange("p (c j a b) -> p c j a b", c=4, j=2, a=4)
                vzm = vz.rearrange("p (c j n) -> p c j n", c=4, j=2)

                def b_part1(h):
                    for j in range(2):
                        b = nextpf()
                        for k in range(8):
                            mm(pf[b][:, :], ring[s2][:, k, j * 128:(j + 1) * 128], xT[:, k, h * 512:(h + 1) * 512], k == 0, k == 7,
                               [('w', s2)] + xT_res(h), [('pf', b)])
                        act(u_t[:, j, :], pf[b][:, :], AF.Gelu, [('pf', b)], [('u', j)])
                    for cc in range(4):
                        c = 4 * h + cc
                        b = nextpf()
                        for k in range(8):
                            mm(pf[b][:, 0:256], xT[:, k, c * 128:(c + 1) * 128], ring[s2][:, k, 256:512], k == 0, k == 7,
                               [('w', s2), ('xT', c)], [('pf', b)])
                        Vr = [('vg', cc)]
                        Vs = [('vst', cc)]
                        act(vg[:, cc, :], pf[b][:, 0:256], AF.Gelu, [('pf', b)], Vr)
                        p.op('dve', lambda e, cc=cc: e.bn_stats(out=vst[:, cc, :], in_=vg[:, cc, :]), Vr, Vs)
                        p.op('dve', lambda e, cc=cc: e.bn_aggr(out=vmv[:, cc, 0:2], in_=vst[:, cc, :]), Vs, Vs)
                        act(vmv[:, cc, 2:3], vmv[:, cc, 1:2], AF.Sqrt, Vs + ['epst'], Vs, bias=epst[:, 0:1])
                        p.op('dve', lambda e, cc=cc: e.reciprocal(out=vmv[:, cc, 3:4], in_=vmv[:, cc, 2:3]), Vs, Vs)
                        ts('dve', vg[:, cc, :], vg[:, cc, :], vmv[:, cc, 0:1], vmv[:, cc, 3:4], ALU.subtract, ALU.mult, Vr + Vs, Vr)
                        tt('pool', vg[:, cc, :], vg[:, cc, :], lvgb[:, 0, :], ALU.mult, Vr + ['lvgb'], Vr)
                        tt('pool', vzv[:, cc, :, 0:4:3, :], vg[:, cc, :].rearrange("p (j a b) -> p j a b", j=2, a=2),
                           lvgb[:, 1, :].rearrange("p (j a b) -> p j a b", j=2, a=2), ALU.add, Vr + ['lvgb', 'vz'], [('vzc', cc)])

                def b_part2(h):
                    for cc in range(4):
                        c = 4 * h + cc
                        b2 = nextpf()
                        for j in range(2):
                            for hp in range(2):
                                mm(pf[b2][:, j * 128:(j + 1) * 128], vzm[:, cc, j, hp * 128:(hp + 1) * 128], wsT[:, l * 4 + 2 * j + hp, :],
                                   hp == 0, hp == 1, [('vzc', cc), 'wsT'], [('pf', b2)])
                        tt('dve', tmpb, pf[b2][:, 0:256], bst[:, :, :].rearrange("p j n -> p (j n)"), ALU.add,
                           [('pf', b2), 'bst'], ['tmpb'])
                        tt('dve', cat[:, 3:5, c * 128:(c + 1) * 128], tmpb.rearrange("p (j n) -> p j n", j=2),
                           u_t[:, :, cc * 128:(cc + 1) * 128], ALU.mult, ['tmpb', ('u', 0), ('u', 1)],
                           [('cat', 3, c), ('cat', 4, c)])

                def conv_a(h):
                    for i in range(3):
                        b = nextpf()
                        rr = [('glu', i, 0)] + ([('glu', i, 'halo')] if h == 0 else [('glu', i, 1)])
                        for k in range(31):
                            mm(pf[b][:, :], dg[:, i * 31 + k, :], glu[:, i, h * 512 + k:h * 512 + k + 512], k == 0, k == 30,
                               rr + [('dg', i, k)], [('pf', b)])
                        act(a_t[:, i, :], pf[b][:, :], AF.Identity, [('pf', b), 'misc'], [('a', i)], bias=misc[:, i, l:l + 1])
                        act(asq[:, i, :], pf[b][:, :], AF.Square, [('pf', b), 'misc'], [('asq', i)], bias=misc[:, i, l:l + 1])
                    b1 = nextpf()
                    b2 = nextpf()
                    for i in range(3):
                        mm(pf[b1][:, :], onesf[:], a_t[:, i, :], i == 0, i == 2, ['onesf', ('a', i)], [('pf', b1)])
                    for i in range(3):
                        mm(pf[b2][:, :], onesf[:], asq[:, i, :], i == 0, i == 2, ['onesf', ('asq', i)], [('pf', b2)])
                    ts('dve', mu_t, pf[b1][:, :], 1.0 / 384, None, ALU.mult, None, [('pf', b1)], ['mu'])
                    tt('dve', var_t, mu_t, mu_t, ALU.mult, ['mu'], ['var'])
                    stt(var_t, pf[b2][:, :], 1.0 / 384, var_t, ALU.mult, ALU.subtract, [('pf', b2), 'var'], ['var'])
                    act(rsd_t, var_t, AF.Sqrt, ['var', 'epst'], ['rsd'], bias=epst[:, 0:1])
                    p.op('dve', lambda e: e.reciprocal(out=rsd_t, in_=rsd_t), ['rsd'], ['rsd'])

                def conv_a_tail(h):
                    for i in range(3):
                        tt('dve', a_t[:, i, :], a_t[:, i, :], mu_t, ALU.subtract, [('a', i), 'mu'], [('a', i)])
                        tt('dve', a_t[:, i, :], a_t[:, i, :], rsd_t, ALU.mult, [('a', i), 'rsd'], [('a', i)])
                        act(cat[:, i, h * 512:(h + 1) * 512], a_t[:, i, :], AF.Silu, [('a', i), 'misc'],
                            [('cat', i, 4 * h + cc) for cc in range(4)],
                            bias=misc[:, i, 8 + l:8 + l + 1], scale=misc[:, i, 4 + l:4 + l + 1])

                b_part1(0)
                flush_pdefer(4)
                conv_a(0)
                b_part2(0)
                b_part1(1)
                flush_pdefer()
                conv_a_tail(0)
                conv_a(1)
                b_part2(1)
                conv_a_tail(1)
                s4 = wget(('in', l, 4))
                s5 = wget(('in', l, 5))
                for h in range(2):
                    for i in range(3):
                        bc_ = nextpf()
                        bx = nextpf()
                        for k in range(8):
                            mm(pf[bc_][:, :], ring[s4][:, k, i * 128:(i + 1) * 128], xT[:, k, h * 512:(h + 1) * 512], k == 0, k == 7,
                               [('w', s4)] + xT_res(h), [('pf', bc_)])
                        for k in range(8):
                            mm(pf[bx][:, :], ring[s5][:, k, i * 128:(i + 1) * 128], xT[:, k, h * 512:(h + 1) * 512], k == 0, k == 7,
                               [('w', s5)] + xT_res(h), [('pf', bx)])
                        sb_ = (h * 3 + i) % 2
                        act(sigt[:, sb_, :], pf[bx][:, :], AF.Copy, [('pf', bx)], [('sigt', sb_)])
                        tt('dve', prod[:, i, 2 + h * 512:2 + (h + 1) * 512], pf[bc_][:, :], sigt[:, sb_, :], ALU.mult,
                           [('pf', bc_), ('sigt', sb_)], [('prod', i, h)])
                cp('pool', pstate[:, l, :, :], prod[:, :, 1024:1026], [('prod', i, 1) for i in range(3)], [('pstate', l)])
                s3 = wget(('in', l, 3))
                for h in range(2):
                    for i in range(3):
                        bb = nextpf()
                        bk = nextpf()
                        for k in range(8):
                            mm(pf[bb][:, :], ring[s3][:, k, i * 128:(i + 1) * 128], xT[:, k, h * 512:(h + 1) * 512], k == 0, k == 7,
                               [('w', s3)] + xT_res(h), [('pf', bb)])
                        rr = [('prod', i, 0)] + ([('prod', i, 'halo')] if h == 0 else [('prod', i, 1)])
                        for k in range(3):
                            mm(pf[bk][:, :], dgc[:, i * 3 + k, :], prod[:, i, h * 512 + k:h * 512 + k + 512], k == 0, k == 2,
                               rr + [('dgc', i, k)], [('pf', bk)])
                        sb_ = (h * 3 + i) % 2
                        act(sigt[:, sb_, :], pf[bk][:, :], AF.Copy, [('pf', bk)], [('sigt', sb_)])
                        tt('dve', cat[:, 5 + i, h * 512:(h + 1) * 512], pf[bb][:, :], sigt[:, sb_, :], ALU.mult,
                           [('pf', bb), ('sigt', sb_)], [('cat', 5 + i, 4 * h + cc) for cc in range(4)])
                umark()
                umark('B')
                load_gbt(l, 0)

                lp1 = LNPipe(True, (0 * 4 + l, 1 * 4 + l))

                def after1(hf, c):
                    if hf == 1:
                        lp1.push(c)
                proj_out('out', l, after1)
                lp1.flush()
                k_proj(l)
                make_xT(0, 0 * 4 + l, 1 * 4 + l)
                v_proj(l)

                sq = [wget(('q', l, 0)), wget(('q', l, 1))]
                evq = [0]

                def q_group(blk, j, h):
                    b = nextpf()
                    s_ = sq[blk]
                    for k in range(8):
                        mm(pf[b][:, :], ring[s_][:, k, j * 128:(j + 1) * 128], xT[:, k, h * 512:(h + 1) * 512], k == 0, k == 7,
                           [('w', s_)] + xT_res(h), [('pf', b)])
                    if evq[0] % 2 == 0:
                        act(qT[:, blk * 4 + j, h * 512:(h + 1) * 512], pf[b][:, :], AF.Copy, [('pf', b)], [('qT', blk * 4 + j, h)])
                    else:
                        cp('dve', qT[:, blk * 4 + j, h * 512:(h + 1) * 512], pf[b][:, :], [('pf', b)], [('qT', blk * 4 + j, h)])
                    evq[0] += 1
                flush_pdefer()
                for blk in range(2):
                    for j in range(4):
                        q_group(blk, j, 0)
                make_xT(1, 0 * 4 + l, 1 * 4 + l)

                def att_scores(h, cc):
                    c = 4 * h + cc
                    par = cc % 2
                    bs_ = [nextpf(), nextpf()]
                    for hd in range(4):
                        for dc in range(2):
                            mm(pf[bs_[hd // 2]][:, (hd % 2) * 256:(hd % 2 + 1) * 256], qT[:, hd * 2 + dc, c * 128:(c + 1) * 128],
                               KT[:, hd * 2 + dc, :], dc == 0, dc == 1, [('qT', hd * 2 + dc, h), 'KT'], [('pf', bs_[hd // 2])])
                    A = [('att', par)]
                    for q2 in range(2):
                        p.op('dve', lambda e, q2=q2, par=par, bq=bs_[q2]: e.tensor_reduce(
                            out=amx[:, par, q2 * 2:q2 * 2 + 2], in_=pf[bq][:, :].rearrange("p (h n) -> p h n", h=2),
                            axis=AX.X, op=ALU.max), [('pf', bs_[q2])], A)
                    ts('dve', amx[:, par, 4:8], amx[:, par, 0:4], -1.0 / 16, None, ALU.mult, None, A, A)
                    for hd in range(4):
                        act(pe_[:, par, hd, :], pf[bs_[hd // 2]][:, (hd % 2) * 256:(hd % 2 + 1) * 256], AF.Exp,
                            [('pf', bs_[hd // 2])] + A, [('pe', par, hd)] + A,
                            bias=amx[:, par, 4 + hd:5 + hd], scale=1.0 / 16, accum=asum[:, par, hd:hd + 1])
                    p.op('dve', lambda e, par=par: e.reciprocal(out=asum[:, par, 4:8], in_=asum[:, par, 0:4]), A, A)
                    for hd in range(4):
                        ts('dve', pn[:, par, hd, :], pe_[:, par, hd, :], asum[:, par, 4 + hd:5 + hd], None, ALU.mult, None,
                           [('pe', par, hd)] + A, [('pn', par)])

                def att_tr(h, cc):
                    par = cc % 2
                    b = nextpb()
                    for hd in range(4):
                        for mc in range(2):
                            tr(pb[b][:, (hd * 2 + mc) * 128:(hd * 2 + mc + 1) * 128], pn[:, par, hd, mc * 128:(mc + 1) * 128], identb[:],
                               [('pn', par), 'identb'], [('pb', b)])
                    if cc % 2 == 0:
                        act(pT[:, h, :, :, cc * 128:(cc + 1) * 128], pb[b][:, :].rearrange("p (h m n) -> p h m n", h=4, m=2), AF.Copy,
                            [('pb', b)], [('pT', h, cc)])
                    else:
                        cp('dve', pT[:, h, :, :, cc * 128:(cc + 1) * 128], pb[b][:, :].rearrange("p (h m n) -> p h m n", h=4, m=2),
                           [('pb', b)], [('pT', h, cc)])

                def pv_group(h, g8):
                    hd, dc = g8 // 2, g8 % 2
                    b = nextpf()
                    for mc in range(2):
                        mm(pf[b][:, :], Vt[:, mc, hd * 256 + dc * 128:hd * 256 + (dc + 1) * 128], pT[:, h, hd, mc, :], mc == 0, mc == 1,
                           ['Vt'] + [('pT', h, cc) for cc in range(4)], [('pf', b)])
                    wr = [('cat', hd * 2 + dc, 4 * h + cc) for cc in range(4)]
                    if dc == 0:
                        act(cat[:, hd * 2 + dc, h * 512:(h + 1) * 512], pf[b][:, :], AF.Copy, [('pf', b)], wr)
                    else:
                        cp('dve', cat[:, hd * 2 + dc, h * 512:(h + 1) * 512], pf[b][:, :], [('pf', b)], wr)

                seqc = [(h, cc) for h in range(2) for cc in range(4)]
                qfill = [(blk, j) for blk in range(2) for j in range(4)]
                qsched = {0: qfill[0:2], 1: qfill[2:4], 2: qfill[4:6], 3: qfill[6:8]}
                att_scores(*seqc[0])
                for idx in range(8):
                    h, cc = seqc[idx]
                    if h == 0:
                        for (blk, j) in qsched[cc]:
                            q_group(blk, j, 1)
                    if idx + 1 < 8:
                        att_scores(*seqc[idx + 1])
                    if h == 1:
                        pv_group(0, 2 * cc)
                        pv_group(0, 2 * cc + 1)
                    att_tr(h, cc)
                for g8 in range(8):
                    pv_group(1, g8)
                umark()
                load_gbt(l, 1)

                lp2 = LNPipe(True, (2 * 4 + l, 3 * 4 + l))

                def after2(hf, c):
                    if hf == 1:
                        lp2.push(c)
                proj_out('o', l, after2)
                lp2.flush()
                make_xT(0, 2 * 4 + l, 3 * 4 + l)


                def store_chunk(c, ti=ti, row0=row0):
                    dma('sp', out[row0 + c * 128:row0 + (c + 1) * 128, :], x_tm[:, c, :],
                        [('x', c)], [('o', ti, c)], f'xout{c}')
                    if ti + 1 < NT:
                        r1 = (ti + 1) * T
                        dma('sp', x_tm[:, c, :], x[r1 + c * 128:r1 + (c + 1) * 128, :], [], [('x', c)], f'xin{c}')
                lp3 = LNPipe(not lastl, None if lastl else (4 * 4 + l, 5 * 4 + l), store_chunk if lastl else None)
                ri = 0
                for g in range(2):
                    if g == 1:
                        load_gbt(l, 2)
                    for blk in range(4):
                        s = wget(('f1', l, g, blk))
                        if g == 0 and blk in (1, 2):
                            flush_pdefer(4)
                        if g == 0 and blk == 0:
                            order = [(j, 0) for j in range(4)] + ['x1'] + [(j, 1) for j in range(4)]
                        else:
                            order = [(j, h) for j in range(4) for h in range(2)]
                        for it in order:
                            if it == 'x1':
                                make_xT(1, 2 * 4 + l, 3 * 4 + l)
                                continue
                            j, h = it
                            if True:
                                b = nextpf()
                                for k in range(8):
                                    mm(pf[b][:, :], ring[s][:, k, j * 128:(j + 1) * 128], xT[:, k, h * 512:(h + 1) * 512], k == 0, k == 7,
                                       [('w', s)] + xT_res(h), [('pf', b)])
                                rb = ri % 2
                                ri += 1
                                act(rt[:, rb, :], pf[b][:, :], AF.Relu, [('pf', b)], [('rt', rb)])
                                tt('dve', hT[:, blk * 4 + j, h * 512:(h + 1) * 512], rt[:, rb, :], pf[b][:, :], ALU.mult,
                                   [('rt', rb), ('pf', b)], [('hT', blk * 4 + j, h)])
                    for cb in range(2):
                        sa = wget(('f2', l, g, cb, 0))
                        sb2 = wget(('f2', l, g, cb, 1))
                        for c in range(NCH):
                            b = nextpf()
                            for kk in range(2):
                                s = sa if kk == 0 else sb2
                                for k in range(8):
                                    mm(pf[b][:, :], hT[:, kk * 8 + k, c * 128:(c + 1) * 128], ring[s][:, k, 0:512],
                                       kk == 0 and k == 0, kk == 1 and k == 7,
                                       [('w', s), ('hT', kk * 8 + k, c // 4)], [('pf', b)])
                            xs = x_tm[:, c, cb * 512:(cb + 1) * 512]
                            if g == 0:
                                stt(xs, xs, ALPHA, pf[b][:, :], ALU.mult, ALU.add, [('x', c), ('pf', b)], [('x', c)])
                            else:
                                tt('dve', xs, xs, pf[b][:, :], ALU.add, [('x', c), ('pf', b)], [('x', c)])
                            if g == 1 and cb == 1:
                                lp3.push(c)
                lp3.flush()
                if not lastl:
                    make_xT(0, 4 * 4 + l, 5 * 4 + l)
                    pend['x1'] = (4 * 4 + l, 5 * 4 + l)
                umark()
        p.op('sp', None, [('o', ti, c) for ti in range(NT) for c in range(NCH)], [])
        assert wstate['use'] == len(specs)
        p.emit(nc, stack)
    nc._dbg = p.dbg
    return nc


WNAMES = ['w_in', 'conv_a_w', 'conv_a_b', 'ln_a_g', 'ln_a_b', 'ln_v_g', 'ln_v_b', 'w_s', 'b_s', 'conv_c_w', 'w_out',
          'ln1_g', 'ln1_b', 'w_q', 'w_kv', 'w_o', 'ln2_g', 'ln2_b', 'w_ff1', 'w_ff2', 'ln3_g', 'ln3_b']


def prep_weights(inputs):
    wd = {}
    for n in WNAMES:
        a = np.ascontiguousarray(np.asarray(inputs[n], dtype=np.float32))
        if n == 'conv_a_w':
            a = a.reshape(DEPTH * 31, 384)
        elif n == 'conv_c_w':
            a = a.reshape(DEPTH * 3, 384)
        wd[n] = a
    return wd


def kernel(**inputs):
    ncores = 8
    x = np.asarray(inputs['x'], dtype=np.float32)
    mem = np.asarray(inputs['mem'], dtype=np.float32)
    B = x.shape[0]
    nseq = B // ncores
    wd = prep_weights(inputs)
    nc = build(nseq, DEPTH)
    in_maps = []
    for c in range(ncores):
        m = dict(wd)
        m['x'] = np.ascontiguousarray(x[c * nseq:(c + 1) * nseq].reshape(nseq * S, D))
        m['mem'] = np.ascontiguousarray(mem[c * nseq:(c + 1) * nseq].reshape(nseq * MEM, D))
        in_maps.append(m)
    res = run_bass_kernel_spmd(nc, in_maps, core_ids=list(range(ncores)))
    outs = [np.asarray(r['out']).reshape(nseq, S, D) for r in res.results]
    return np.concatenate(outs, axis=0).astype(np.float32)
```
